# Optimizing a Trainium2 kernel written in Bass

```python
import math
import jax, jax.numpy as jnp
from jax import lax
import numpy as np

D_MODEL = 1024
BATCH = 4
SEQ = 8192
DEPTH = 1

GLA_HEADS = 4
GLA_DK = D_MODEL // 16
GLA_DV = D_MODEL // 8
GLA_LOWRANK = 16
GLA_TAU = 16.0
GLA_CHUNK = 64
MOBA_HEADS = 4
MOBA_DH = 128
MOBA_BLOCK = 256
MOBA_TOPK = 3
MOBA_QCHUNK = 32
NUM_BUCKETS = 32
MAX_DISTANCE = 128
MAX_EXACT = NUM_BUCKETS // 2
D_FF = 2816
FFN_RES = 0.5
EPS = 1e-6
N_MOD = 9
GLA_QK_W = GLA_HEADS * GLA_DK
GLA_V_W = GLA_HEADS * GLA_DV
MOBA_W = MOBA_HEADS * MOBA_DH
IN_SIZES = (GLA_QK_W, GLA_QK_W, GLA_V_W, GLA_LOWRANK, GLA_V_W, MOBA_W, MOBA_W, MOBA_W, D_MODEL, D_MODEL)
IN_WIDTH = sum(IN_SIZES)

kernel_name = "hybrid_gla_moba_macaron_adaln"


def rmsnorm(x, g):
    xf = x.astype(jnp.float32)
    y = xf * lax.rsqrt(jnp.mean(xf * xf, axis=-1, keepdims=True) + EPS) * g.astype(jnp.float32)
    return y.astype(x.dtype)


def modulate(x, shift, scale):
    return x * (1.0 + scale[:, None, :]) + shift[:, None, :]


def swiglu(x, w_gate, w_up, w_down):
    return (jax.nn.silu(x @ w_gate) * (x @ w_up)) @ w_down


def rel_bucket(dist):
    n = jnp.maximum(dist, 0)
    nf = jnp.maximum(n, 1).astype(jnp.float32)
    large = MAX_EXACT + (jnp.log(nf / MAX_EXACT) / math.log(MAX_DISTANCE / MAX_EXACT)
                         * (NUM_BUCKETS - MAX_EXACT)).astype(jnp.int32)
    large = jnp.minimum(large, NUM_BUCKETS - 1)
    return jnp.where(n < MAX_EXACT, n, large)


def gla_chunked(q, k, v, log_a):
    B, H, S, dk = q.shape
    dv = v.shape[-1]
    nc = S // GLA_CHUNK

    def to_chunks(t):
        return jnp.moveaxis(t.astype(jnp.float32).reshape(B, H, nc, GLA_CHUNK, t.shape[-1]), 2, 0)

    causal = jnp.tril(jnp.ones((GLA_CHUNK, GLA_CHUNK), dtype=bool))

    def step(state, inp):
        qc, kc, vc, gc = inp
        b = jnp.cumsum(gc, axis=2)
        o_inter = jnp.einsum('bhcd,bhde->bhce', qc * jnp.exp(b), state)
        diff = b[:, :, :, None, :] - b[:, :, None, :, :]
        decay = jnp.exp(jnp.where(causal[:, :, None], diff, -jnp.inf))
        attn = jnp.einsum('bhid,bhijd,bhjd->bhij', qc, decay, kc)
        o = o_inter + jnp.einsum('bhij,bhje->bhie', attn, vc)
        b_last = b[:, :, -1:, :]
        state = (jnp.exp(b_last[:, :, 0, :])[..., None] * state
                 + jnp.einsum('bhcd,bhce->bhde', kc * jnp.exp(b_last - b), vc))
        return state, o

    state0 = jnp.zeros((B, H, dk, dv), jnp.float32)
    _, o = lax.scan(step, state0, (to_chunks(q), to_chunks(k), to_chunks(v), to_chunks(log_a)))
    return jnp.moveaxis(o, 0, 2).reshape(B, H, S, dv)


def moba_attention(q, k, v, rel_bias):
    B, H, S, dh = q.shape
    nb = -(-S // MOBA_BLOCK)
    s_pad = nb * MOBA_BLOCK
    pad = ((0, 0), (0, 0), (0, s_pad - S), (0, 0))
    q = q.astype(jnp.float32)
    k = jnp.pad(k.astype(jnp.float32), pad)
    v = jnp.pad(v.astype(jnp.float32), pad)
    k_blocks = k.reshape(B, H, nb, MOBA_BLOCK, dh)
    v_blocks = v.reshape(B, H, nb, MOBA_BLOCK, dh)
    k_mean = jnp.mean(k_blocks, axis=3)
    topk = min(MOBA_TOPK, nb)
    scale = dh ** -0.5
    head_ix = jnp.arange(H)[None, :, None, None, None]
    gather = jax.vmap(jax.vmap(lambda blocks, ix: blocks[ix]))

    def chunk(ci):
        start = ci * MOBA_QCHUNK
        q_c = lax.dynamic_slice_in_dim(q, start, MOBA_QCHUNK, axis=2)
        q_pos = start + jnp.arange(MOBA_QCHUNK)
        cur = start // MOBA_BLOCK
        gate = jnp.einsum('bhqd,bhnd->bhqn', q_c, k_mean)
        gate = jnp.where(jnp.arange(nb) < cur, gate, -jnp.inf)
        _, sel = lax.top_k(gate, topk)
        sel_valid = jnp.arange(topk) < cur
        k_sel = gather(k_blocks, sel)
        v_sel = gather(v_blocks, sel)
        sel_pos = sel[..., None] * MOBA_BLOCK + jnp.arange(MOBA_BLOCK)
        sel_bias = rel_bias[head_ix, rel_bucket(q_pos[:, None, None] - sel_pos)]
        s_sel = jnp.einsum('bhqd,bhqrkd->bhqrk', q_c, k_sel) * scale + sel_bias
        s_sel = jnp.where(sel_valid[:, None], s_sel, -jnp.inf)
        own_start = cur * MOBA_BLOCK
        k_own = lax.dynamic_slice_in_dim(k, own_start, MOBA_BLOCK, axis=2)
        v_own = lax.dynamic_slice_in_dim(v, own_start, MOBA_BLOCK, axis=2)
        rel = q_pos[:, None] - (own_start + jnp.arange(MOBA_BLOCK))[None, :]
        s_own = jnp.einsum('bhqd,bhkd->bhqk', q_c, k_own) * scale + rel_bias[:, rel_bucket(rel)]
        s_own = jnp.where(rel >= 0, s_own, -jnp.inf)
        n_sel = topk * MOBA_BLOCK
        logits = jnp.concatenate([s_sel.reshape(B, H, MOBA_QCHUNK, n_sel), s_own], axis=-1)
        p = jax.nn.softmax(logits, axis=-1)
        p_sel = p[..., :n_sel].reshape(B, H, MOBA_QCHUNK, topk, MOBA_BLOCK)
        p_own = p[..., n_sel:]
        return (jnp.einsum('bhqrk,bhqrkd->bhqd', p_sel, v_sel)
                + jnp.einsum('bhqk,bhkd->bhqd', p_own, v_own))

    out = lax.map(chunk, jnp.arange(S // MOBA_QCHUNK))
    return jnp.transpose(out, (1, 2, 0, 3, 4)).reshape(B, H, S, dh)


def token_mixing(u, w_in, w_gla_lr, b_gla_lr, gla_norm, rel_bias, w_br_gla, w_br_moba, w_out):
    B, S, _ = u.shape
    proj = u @ w_in
    offsets = np.cumsum(IN_SIZES)[:-1].tolist()
    gq, gk, gv, glr, gog, mq, mk, mv, ga, gb = jnp.split(proj, offsets, axis=-1)

    def heads(t, n):
        return t.reshape(B, S, n, -1).transpose(0, 2, 1, 3)

    log_a = jax.nn.log_sigmoid((glr @ w_gla_lr + b_gla_lr).astype(jnp.float32)) / GLA_TAU
    o_a = gla_chunked(heads(gq, GLA_HEADS) * (GLA_DK ** -0.5), heads(gk, GLA_HEADS),
                      heads(gv, GLA_HEADS), heads(log_a, GLA_HEADS))
    o_a = rmsnorm(o_a.transpose(0, 2, 1, 3), gla_norm)
    o_a = o_a * jax.nn.silu(gog.reshape(B, S, GLA_HEADS, GLA_DV).astype(jnp.float32))
    y_a = o_a.reshape(B, S, GLA_V_W).astype(u.dtype) @ w_br_gla

    o_b = moba_attention(heads(mq, MOBA_HEADS), heads(mk, MOBA_HEADS), heads(mv, MOBA_HEADS), rel_bias)
    y_b = o_b.transpose(0, 2, 1, 3).reshape(B, S, MOBA_W).astype(u.dtype) @ w_br_moba

    merged = jax.nn.sigmoid(ga) * y_a + jax.nn.sigmoid(gb) * y_b
    return merged @ w_out


def setup_inputs(seed: int = 0) -> dict:
    key = jax.random.key(seed)
    ks = jax.random.split(key, 24)
    L, D, F = DEPTH, D_MODEL, D_FF
    nrm = lambda k, shape, fan_in: jax.random.normal(k, shape, jnp.float32) * (fan_in ** -0.5)
    gain = lambda k, shape: 1.0 + 0.05 * jax.random.normal(k, shape, jnp.float32)
    return {
        "x": jax.random.normal(ks[0], (BATCH, SEQ, D), jnp.float32),
        "c": jax.random.normal(ks[1], (BATCH, D), jnp.float32),
        "w_ada": nrm(ks[2], (L, D, N_MOD * D), D) * 0.5,
        "b_ada": 0.01 * jax.random.normal(ks[3], (L, N_MOD * D), jnp.float32),
        "norm_ff1": gain(ks[4], (L, D)),
        "w_ff1_gate": nrm(ks[5], (L, D, F), D),
        "w_ff1_up": nrm(ks[6], (L, D, F), D),
        "w_ff1_down": nrm(ks[7], (L, F, D), F),
        "norm_mix": gain(ks[8], (L, D)),
        "w_in": nrm(ks[9], (L, D, IN_WIDTH), D),
        "w_gla_lr": nrm(ks[10], (L, GLA_LOWRANK, GLA_QK_W), GLA_LOWRANK),
        "b_gla_lr": 0.1 * jax.random.normal(ks[11], (L, GLA_QK_W), jnp.float32),
        "gla_norm": gain(ks[12], (L, GLA_DV)),
        "rel_bias": 0.5 * jax.random.normal(ks[13], (MOBA_HEADS, NUM_BUCKETS), jnp.float32),
        "w_br_gla": nrm(ks[14], (L, GLA_V_W, D), GLA_V_W),
        "w_br_moba": nrm(ks[15], (L, MOBA_W, D), MOBA_W),
        "w_out": nrm(ks[16], (L, D, D), D),
        "norm_ff2": gain(ks[17], (L, D)),
        "w_ff2_gate": nrm(ks[18], (L, D, F), D),
        "w_ff2_up": nrm(ks[19], (L, D, F), D),
        "w_ff2_down": nrm(ks[20], (L, F, D), F),
        "norm_final": gain(ks[21], (D,)),
    }


def reference(x, c, w_ada, b_ada, norm_ff1, w_ff1_gate, w_ff1_up, w_ff1_down, norm_mix, w_in,
              w_gla_lr, b_gla_lr, gla_norm, rel_bias, w_br_gla, w_br_moba, w_out, norm_ff2,
              w_ff2_gate, w_ff2_up, w_ff2_down, norm_final):
    h = x
    c_act = jax.nn.silu(c)
    for l in range(DEPTH):
        mod = c_act @ w_ada[l] + b_ada[l]
        sh1, sc1, g1, sh2, sc2, g2, sh3, sc3, g3 = jnp.split(mod, N_MOD, axis=-1)
        u = modulate(rmsnorm(h, norm_ff1[l]), sh1, sc1)
        h = h + FFN_RES * g1[:, None, :] * swiglu(u, w_ff1_gate[l], w_ff1_up[l], w_ff1_down[l])
        u = modulate(rmsnorm(h, norm_mix[l]), sh2, sc2)
        h = h + g2[:, None, :] * token_mixing(u, w_in[l], w_gla_lr[l], b_gla_lr[l], gla_norm[l], rel_bias,
                                              w_br_gla[l], w_br_moba[l], w_out[l])
        u = modulate(rmsnorm(h, norm_ff2[l]), sh3, sc3)
        h = h + FFN_RES * g3[:, None, :] * swiglu(u, w_ff2_gate[l], w_ff2_up[l], w_ff2_down[l])
    return rmsnorm(h, norm_final)
```

```python
import numpy as np
import math
import concourse.bass as bass
import concourse.mybir as mybir
from concourse.bass_utils import run_bass_kernel_spmd

F32 = mybir.dt.float32
BF16 = mybir.dt.bfloat16
AF = mybir.ActivationFunctionType
ALU = mybir.AluOpType

D = 1024
SEQ = 8192
NB = 4
F = 2816
NFC = 22
TT_ = 512
NT_FULL = SEQ // TT_
EPS = 1e-6
BIG = 30000.0
GLA_TAU = 16.0
N_CORES = 4

OFF = dict(gq=0, gk=256, gv=512, glr=1024, gog=1040, mq=1552, mk=2064, mv=2576, ga=3088, gb=4112)


class T:
    __slots__ = ("w", "r", "name", "ex")

    def __init__(self, name="", ex=False):
        self.w = None
        self.r = {}
        self.name = name
        self.ex = ex


class KB:
    def __init__(self, nc, n_dma_sems=24, same_engine_sync=True):
        self.nc = nc
        self.eng = {"pe": nc.tensor, "act": nc.scalar, "dve": nc.vector, "pool": nc.gpsimd, "sp": nc.sync}
        self.same = same_engine_sync
        self.sem = {}
        self.cnt = {}
        self._cms = []
        for e in self.eng:
            cm = nc.semaphore("prog_" + e)
            self._cms.append(cm)
            self.sem[e] = cm.__enter__()
            self.cnt[e] = 0
        self.dsem = []
        self.dtot = []
        for i in range(n_dma_sems + 8):
            cm = nc.semaphore("dma_%d" % i)
            self._cms.append(cm)
            self.dsem.append(cm.__enter__())
            self.dtot.append(0)
        self.dpool = {"sp": list(range(n_dma_sems)), "pool": list(range(n_dma_sems, n_dma_sems + 8))}
        self.dnext = {"sp": 0, "pool": 0}
        self.seen = {e: {} for e in self.eng}
        self.out_tokens = []

    def close(self):
        for cm in reversed(self._cms):
            cm.__exit__(None, None, None)

    def _wait(self, e, tok):
        sem, val, own = tok
        if own == e and (e == "pe" or not self.same):
            return
        key = id(sem)
        if self.seen[e].get(key, 0) >= val:
            return
        self.eng[e].wait_ge(sem, val)
        self.seen[e][key] = val

    def _deps(self, e, reads, writes):
        toks = []
        for t in reads:
            if t.w is not None:
                toks.append(t.w)
            if t.ex:
                toks.extend(tk for tk in t.r.values() if tk[2] != e)
        for t in writes:
            if t.w is not None:
                toks.append(t.w)
            toks.extend(t.r.values())
        best = {}
        for tok in toks:
            k = id(tok[0])
            if k not in best or best[k][1] < tok[1]:
                best[k] = tok
        for tok in best.values():
            self._wait(e, tok)

    def _commit(self, tok, reads, writes):
        k = id(tok[0])
        for t in reads:
            t.r[k] = tok
        for t in writes:
            t.w = tok
            t.r = {}

    def op(self, e, fn, reads=(), writes=()):
        self._deps(e, reads, writes)
        self.cnt[e] += 1
        tok = (self.sem[e], self.cnt[e], e)
        fn(self.eng[e]).then_inc(self.sem[e], 1)
        self._commit(tok, reads, writes)

    def group(self, e, fns, reads=(), writes=()):
        self._deps(e, reads, writes)
        self.cnt[e] += 1
        tok = (self.sem[e], self.cnt[e], e)
        for fn in fns[:-1]:
            fn(self.eng[e])
        fns[-1](self.eng[e]).then_inc(self.sem[e], 1)
        self._commit(tok, reads, writes)

    def dma(self, q, out_ap, in_ap, reads=(), writes=(), is_output=False):
        pl = self.dpool[q]
        i = pl[self.dnext[q]]
        self.dnext[q] = (self.dnext[q] + 1) % len(pl)
        sem = self.dsem[i]
        if self.dtot[i] > 0:
            self._wait(q, (sem, self.dtot[i], "dma"))
        self._deps(q, reads, writes)
        self.dtot[i] += 16
        tok = (sem, self.dtot[i], "dma")
        self.eng[q].dma_start(out=out_ap, in_=in_ap).then_inc(sem, 16)
        self._commit(tok, reads, writes)
        if is_output:
            self.out_tokens.append(tok)

    def finish(self):
        for tok in self.out_tokens:
            self._wait("sp", tok)
        for i, sem in enumerate(self.dsem):
            if self.dtot[i] > 0:
                self._wait("sp", (sem, self.dtot[i], "dma"))


DEBUG_STOP = 99


def build_program(NT):
    STOP = DEBUG_STOP
    nc = bass.Bass("TRN2", target_bir_lowering=False)
    S = NT * TT_
    NBLK = S // 256

    def din(name, shape, dt=F32):
        return nc.dram_tensor(name, list(shape), dt, kind="ExternalInput").ap()

    def dscr(name, shape, dt=BF16):
        return nc.dram_tensor(name, list(shape), dt, kind="Internal").ap()

    xT = din("xT", [128, 8, S])
    cT = din("cT", [128, 8])
    w_ada = din("w_ada", [36, 128, 8 * 256])
    b_ada = din("b_ada", [128, 72])
    gains = din("gains", [128, 32])
    wgu = [din("wgu%d" % i, [11, 128, 4096]) for i in range(2)]
    wdn = [din("wdn%d" % i, [8, 128, NFC * 128]) for i in range(2)]
    win_fm = din("win_fm", [8, 128, 4096])
    win_tm = din("win_tm", [3, 128, 4096])
    win_lr = din("win_lr", [128, 8 * 16])
    wlr = din("wlr", [16, 256])
    blr_fm = din("blr_fm", [128, 2])
    blr_tm = din("blr_tm", [128, 256])
    glan = din("glan", [128, 1])
    wbr = din("wbr", [2, 128, 4096])
    wout = din("wout", [2, 128, 4096])
    relbase = din("relbase", [128, 4 * 1024])
    b31 = din("b31", [128, 4])
    consts = din("consts", [128, 5 * 128])
    esel_in = din("esel", [128, 32 * 128])
    outT = nc.dram_tensor("outT", [128, 8, S], F32, kind="ExternalOutput").ap()

    s_wgu = [dscr("s_wgu%d" % i, [11, 128, 4096]) for i in range(2)]
    s_wdn = [dscr("s_wdn%d" % i, [8, 128, NFC * 128]) for i in range(2)]
    s_fm = dscr("s_fm", [8, 128, 4096])
    s_tm = dscr("s_tm", [3, 128, 4096])
    s_lr = dscr("s_lr", [128, 128])
    s_br = dscr("s_br", [2, 128, 4096])
    s_out = dscr("s_out", [2, 128, 4096])
    s_K = dscr("s_K", [128, 4, S])
    s_V = dscr("s_V", [S, 512])

    import contextlib
    es = contextlib.ExitStack()
    kb = KB(nc)

    def sb(name, shape, dt=F32):
        return es.enter_context(nc.sbuf_tensor(name, list(shape), dt))

    def ps(name):
        return es.enter_context(nc.psum_tensor(name, [128, 512], F32))

    h = sb("h", [128, 8, 512]); h_t = [T("h%d" % i) for i in range(8)]
    u = sb("u", [128, 8, 512], BF16); u_t = [T("u%d" % i) for i in range(8)]
    a = sb("a", [128, NFC, 512], BF16); a_t = [T("a%d" % i) for i in range(NFC)]
    NW = 4
    wsl = [sb("wsl%d" % i, [128, 4096], BF16) for i in range(NW)]; wsl_t = [T("wsl%d" % i) for i in range(NW)]
    rstd = sb("rstd", [128, 512]); rstd_t = T("rstd")
    sq = [sb("sq%d" % i, [128, 512], BF16) for i in range(2)]; sq_t = [T() for _ in range(2)]
    sg = [sb("sg%d" % i, [128, 512]) for i in range(2)]; sg_t = [T() for _ in range(2)]
    qT = sb("qT", [128, 4, 512], BF16); qT_t = [T() for _ in range(4)]
    kT = sb("kT", [128, 4, 512], BF16); kT_t = T("kT")
    Vt = sb("Vt", [128, 4, 512], BF16); Vt_t = T("Vt")
    gqk = sb("gqk", [128, 4, 512]); gqk_t = [T() for _ in range(4)]
    glrT = sb("glrT", [16, 512], BF16); glrT_t = T("glrT")
    ktm = sb("ktm", [128, 4, 256]); ktm_t = T("ktm")
    vtm = sb("vtm", [128, 4, 512], BF16); vtm_t = T("vtm")
    kd = sb("kd", [128, 4, 256], BF16); kd_t = [T() for _ in range(4)]
    la = sb("la", [128, 4, 256]); la_t = [T() for _ in range(4)]
    bcs = sb("bcs", [128, 256]); bcs_t = T("bcs")
    qe = sb("qe", [128, 2, 512], BF16); qe_t = [T() for _ in range(2)]
    ki = sb("ki", [128, 2, 512], BF16); ki_t = [T() for _ in range(2)]
    ebl = sb("ebl", [128, 2, 8]); ebl_t = [T() for _ in range(2)]
    attn = [sb("attn%d" % i, [128, 128], BF16) for i in range(2)]; attn_t = [T() for _ in range(2)]
    Sm = sb("Sm", [128, 4, 128]); Sm_t = [T() for _ in range(4)]
    Sb = sb("Sb", [128, 4, 128], BF16); Sb_t = [T() for _ in range(4)]
    obT = sb("obT", [128, 4, 512], BF16); obT_t = [T() for _ in range(4)]
    pt = [sb("pt%d" % i, [128, 512], BF16) for i in range(6)]; pt_t = [T() for _ in range(6)]
    ssb = [sb("ssb%d" % i, [128, 512]) for i in range(2)]; ssb_t = [T() for _ in range(2)]
    e12, e12_t = ssb, ssb_t
    tmp, tmp_t = ssb, ssb_t
    NKV = 3
    kvK = [sb("kvK%d" % i, [128, 2, 512], BF16) for i in range(NKV)]; kvK_t = [T() for _ in range(NKV)]
    kvV = [sb("kvV%d" % i, [128, 4, 256], BF16) for i in range(NKV)]; kvV_t = [T() for _ in range(NKV)]
    base = sb("base", [128, 4 * 1024]); base_t = T("base")
    b31s = sb("b31s", [128, 4]); b31_t = T("b31")
    cst = sb("cst", [128, 5 * 128]); cst_t = T("cst")
    cbf = sb("cbf", [128, 5 * 128], BF16); cbf_t = T("cbf")
    esel = sb("eselb", [128, 32 * 128], BF16); esel_t = T("esel")
    kmean = sb("kmean", [128, 4, 32], BF16); kmean_t = T("kmean")
    kms = sb("kms", [128, 8]); kms_t = T("kms")
    gsb2 = [sb("gsb%d" % i, [128, 32]) for i in range(2)]; gsb2_t = [T() for _ in range(2)]
    m82 = [sb("m8_%d" % i, [128, 8]) for i in range(2)]; m82_t = [T() for _ in range(2)]
    nmk2 = [sb("nmk%d" % i, [128, 32]) for i in range(2)]; nmk2_t = [T() for _ in range(2)]
    nmT = sb("nmT", [128, 4, 512], BF16); nmT_t = [T() for _ in range(4)]
    mod = sb("mod", [128, 72]); mod_t = T("mod")
    vecs = sb("vecs", [128, 64]); vecs_t = T("vecs")
    gn = sb("gn", [128, 32]); gn_t = T("gn")
    cact = sb("cact", [128, 8]); cact_t = T("cact")
    small = sb("small", [128, 16]); small_t = T("small")
    wlr_s = sb("wlr_s", [16, 256], BF16); wlr_t = T("wlr")
    blrf = sb("blrf", [128, 2]); blrf_t = T("blrf")
    blrt = sb("blrt", [128, 256]); blrt_t = T("blrt")
    glan_s = sb("glan_s", [128, 1]); glan_t = T("glan")
    den, den_t = rstd, rstd_t
    rowb = [sb("rowb%d" % i, [1, 256]) for i in range(2)]; rowb_t = [T() for _ in range(2)]

    P = [ps("ps%d" % i) for i in range(8)]; P_t = [T("ps%d" % i, ex=True) for i in range(8)]

    sga = lambda c: a[:, c, :]; sga_t = lambda c: a_t[c]
    sgb = lambda c: a[:, 8 + c, :]; sgb_t = lambda c: a_t[8 + c]
    oab = lambda hh: a[:, 16 + hh, :]; oab_t = lambda hh: a_t[16 + hh]
    gog = lambda hh: a[:, 20 + hh // 2, (hh % 2) * 256:(hh % 2) * 256 + 256]
    merged = lambda c: u[:, c, :]; merged_t = lambda c: u_t[c]
    gogs = sb("gogs", [128, 4, 512], BF16); gogs_t = [T() for _ in range(4)]

    IDN, TRI, LBK, M2C, ONE = 0, 128, 256, 384, 512
    V_GSC = [0, 8, 16]
    V_HG = [24, 32, 40]
    V_NEGBLR = 48
    M_SH = [0, 24, 48]
    M_SC = [8, 32, 56]
    M_G = [16, 40, 64]

    def act(out, in_, func, reads, writes, bias=None, scale=None):
        kw = {}
        if bias is not None:
            kw["bias"] = bias
        if scale is not None:
            kw["scale"] = scale
        kb.op("act", lambda e: e.activation(out=out, in_=in_, func=func, **kw), reads, writes)

    def tt(out, in0, in1, op, reads, writes, eng="dve"):
        kb.op(eng, lambda e: e.tensor_tensor(out=out, in0=in0, in1=in1, op=op), reads, writes)

    def ts(out, in0, s1, s2, op0, op1, reads, writes, eng="dve"):
        if s2 is None:
            kb.op(eng, lambda e: e.tensor_scalar(out=out, in0=in0, scalar1=s1, scalar2=None, op0=op0), reads, writes)
        else:
            kb.op(eng, lambda e: e.tensor_scalar(out=out, in0=in0, scalar1=s1, scalar2=s2, op0=op0, op1=op1), reads, writes)

    def stt(out, in0, scalar, in1, op0, op1, reads, writes, eng="dve"):
        kb.op(eng, lambda e: e.scalar_tensor_tensor(out=out, in0=in0, scalar=scalar, in1=in1, op0=op0, op1=op1), reads, writes)

    def cp(out, in_, reads, writes, eng="dve"):
        if eng == "act_copy":
            act(out, in_, AF.Identity, reads, writes)
        else:
            kb.op(eng, lambda e: e.tensor_copy(out=out, in_=in_), reads, writes)

    def mms_split(out, items, common_reads, per_reads, writes):
        n = len(items)
        for i, (l, r) in enumerate(items):
            kb.group("pe", [lambda e, l=l, r=r, i=i: e.matmul(out, l, r, start=(i == 0), stop=(i == n - 1))],
                     list(common_reads) + [per_reads[i]], writes)

    def mms(out, items, reads, writes):
        n = len(items)
        fns = []
        for i, (l, r) in enumerate(items):
            fns.append(lambda e, l=l, r=r, i=i: e.matmul(out, l, r, start=(i == 0), stop=(i == n - 1)))
        kb.group("pe", fns, reads, writes)

    def mm_raw(fns, reads, writes):
        kb.group("pe", fns, reads, writes)

    dq = "sp"
    kb.dma(dq, cst[:], consts, writes=[cst_t])
    kb.dma(dq, base[:], relbase, writes=[base_t])
    kb.dma(dq, b31s[:], b31, writes=[b31_t])
    kb.dma(dq, gn[:], gains, writes=[gn_t])
    kb.dma(dq, cact[:], cT, writes=[cact_t])
    kb.dma(dq, mod[:], b_ada, writes=[mod_t])
    kb.dma(dq, blrf[:], blr_fm, writes=[blrf_t])
    kb.dma(dq, blrt[:], blr_tm, writes=[blrt_t])
    kb.dma(dq, glan_s[:], glan, writes=[glan_t])
    kb.dma("pool", wlr_s[:], wlr, writes=[wlr_t])
    kb.dma("pool", esel[:], esel_in, writes=[esel_t])
    cp(cbf[:], cst[:], [cst_t], [cbf_t])
    kb.op("dve", lambda e: e.memset(small[:, 0:1], EPS), (), [small_t])
    kb.op("dve", lambda e: e.memset(small[:, 1:2], 1.0), (), [small_t])
    kb.op("dve", lambda e: e.memset(small[:, 2:3], 0.0), (), [small_t])
    kb.op("dve", lambda e: e.memset(kmean[:], 0.0), (), [kmean_t])
    for hh in range(4):
        kb.op("dve", lambda e, hh=hh: e.memset(nmT[:, hh, :], 0.0), (), [nmT_t[hh]])
    for hh in range(4):
        kb.op("dve", lambda e, hh=hh: e.memset(Sm[:, hh, :], 0.0), (), [Sm_t[hh]])
        kb.op("dve", lambda e, hh=hh: e.memset(Sb[:, hh, :], 0.0), (), [Sb_t[hh]])

    scr_t = {}

    cur_t = [0]

    def wload(dst_ap, key, src_f32, scr_ap):
        if cur_t[0] == 0:
            kb.dma("pool", dst_ap[0], src_f32, writes=[dst_ap[1]])
            t_ = T(str(key))
            scr_t[key] = t_
            kb.dma("pool", scr_ap, src_f32, writes=[t_])
        else:
            kb.dma("pool", dst_ap[0], scr_ap, reads=[scr_t[key]], writes=[dst_ap[1]])

    act(cact[:], cact[:], AF.Silu, [cact_t], [cact_t])
    for pc in range(36):
        s_ = pc % 2
        wt_ = h_t[4 * s_:4 * s_ + 4]
        kb.dma("sp", h[:, 4 * s_:4 * s_ + 4, :], w_ada[pc].rearrange("p (c n) -> p c n", c=4), writes=wt_)
        mms(P[0][0:1, s_ * 256:(s_ + 1) * 256],
            [(cact[:, kc:kc + 1], h[:, 4 * s_ + kc // 2, (kc % 2) * 256:(kc % 2) * 256 + 256]) for kc in range(8)],
            wt_ + [cact_t], [P_t[0]])
        cp(rowb[s_][:], P[0][0:1, s_ * 256:(s_ + 1) * 256], [P_t[0]], [rowb_t[s_]], eng="act_copy")
        for j in range(2):
            col = pc * 2 + j
            mms(P[1][:, col:col + 1], [(rowb[s_][0:1, j * 128:(j + 1) * 128], small[0:1, 1:2])], [rowb_t[s_], small_t], [P_t[1]])
    tt(mod[:], mod[:], P[1][:, 0:72], ALU.add, [mod_t, P_t[1]], [mod_t])
    for i in range(3):
        stt(vecs[:, V_GSC[i]:V_GSC[i] + 8], mod[:, M_SC[i]:M_SC[i] + 8], 1.0, gn[:, i * 8:i * 8 + 8], ALU.add, ALU.mult,
            [mod_t, gn_t], [vecs_t])
    ts(vecs[:, V_HG[0]:V_HG[0] + 8], mod[:, M_G[0]:M_G[0] + 8], 0.5, None, ALU.mult, None, [mod_t], [vecs_t])
    cp(vecs[:, V_HG[1]:V_HG[1] + 8], mod[:, M_G[1]:M_G[1] + 8], [mod_t], [vecs_t])
    ts(vecs[:, V_HG[2]:V_HG[2] + 8], mod[:, M_G[2]:M_G[2] + 8], 0.5, None, ALU.mult, None, [mod_t], [vecs_t])
    ts(vecs[:, V_NEGBLR:V_NEGBLR + 2], blrf[:], -1.0, None, ALU.mult, None, [blrf_t], [vecs_t])

    wcur = [0]
    WQ = "pool"

    def wslot():
        i = wcur[0]
        wcur[0] = (i + 1) % NW
        return i

    ones_bf = cbf[:, ONE:ONE + 128]

    def stat_chunk(c):
        s_ = c % 2
        act(sq[s_][:], h[:, c, :], AF.Square, [h_t[c]], [sq_t[s_]])
        kb.group("pe", [lambda e: e.matmul(P[7][:], ones_bf, sq[s_][:], start=(c == 0), stop=(c == 7))],
                 [sq_t[s_], cbf_t], [P_t[7]])

    def stats_finish():
        act(rstd[:], P[7][:], AF.Ln, [P_t[7], small_t], [rstd_t], bias=small[:, 0:1], scale=1.0 / D)
        act(rstd[:], rstd[:], AF.Exp, [rstd_t], [rstd_t], scale=-0.5)

    def norm_mod(i, pre=False):
        if not pre:
            for c in range(8):
                stat_chunk(c)
        stats_finish()
        for c in range(8):
            s_ = c % 2
            stt(tmp[s_][:], h[:, c, :], vecs[:, V_GSC[i] + c:V_GSC[i] + c + 1], rstd[:], ALU.mult, ALU.mult,
                [h_t[c], vecs_t, rstd_t], [tmp_t[s_]])
            act(u[:, c, :], tmp[s_][:], AF.Identity, [tmp_t[s_], mod_t], [u_t[c]],
                bias=mod[:, M_SH[i] + c:M_SH[i] + c + 1], scale=1.0)

    def ffn(i, which, pre=False):
        norm_mod(i, pre)
        for blk in range(11):
            w_ = wslot()
            wload((wsl[w_][:, 0:4096], wsl_t[w_]), ("wgu", which, blk), wgu[which][blk], s_wgu[which][blk])
            for j in range(2):
                fc = blk * 2 + j
                pg, pu = fc % 2, 2 + fc % 2
                if fc == 0:
                    mms_split(P[pg][:], [(wsl[w_][:, kc * 256 + j * 128:kc * 256 + j * 128 + 128], u[:, kc, :]) for kc in range(8)],
                              [wsl_t[w_]], u_t, [P_t[pg]])
                else:
                    mms(P[pg][:], [(wsl[w_][:, kc * 256 + j * 128:kc * 256 + j * 128 + 128], u[:, kc, :]) for kc in range(8)],
                        [wsl_t[w_]] + u_t, [P_t[pg]])
                mms(P[pu][:], [(wsl[w_][:, 2048 + kc * 256 + j * 128:2048 + kc * 256 + j * 128 + 128], u[:, kc, :]) for kc in range(8)],
                    [wsl_t[w_]] + u_t, [P_t[pu]])
                act(sg[fc % 2][:], P[pg][:], AF.Silu, [P_t[pg]], [sg_t[fc % 2]])
                tt(a[:, fc, :], sg[fc % 2][:], P[pu][:], ALU.mult, [sg_t[fc % 2], P_t[pu]], [a_t[fc]])
        for c in range(8):
            w_ = wslot()
            wload((wsl[w_][:, 0:NFC * 128], wsl_t[w_]), ("wdn", which, c), wdn[which][c], s_wdn[which][c])
            pd = 4 + c % 2
            mms(P[pd][:], [(wsl[w_][:, fc * 128:fc * 128 + 128], a[:, fc, :]) for fc in range(NFC)],
                [wsl_t[w_]] + a_t, [P_t[pd]])
            stt(h[:, c, :], P[pd][:], vecs[:, V_HG[i] + c:V_HG[i] + c + 1], h[:, c, :], ALU.mult, ALU.add,
                [P_t[pd], vecs_t, h_t[c]], [h_t[c]])
            if c >= 1:
                stat_chunk(c - 1)
        stat_chunk(7)

    sK_t = [T("sK%d" % t) for t in range(NT)]
    sV_t = [T("sV%d" % t) for t in range(NT)]
    rr = [0]

    def nextp(lst):
        rr[0] += 1
        return lst[rr[0] % len(lst)]

    def fm_block(blk, evac):
        w_ = wslot()
        wload((wsl[w_][:, 0:4096], wsl_t[w_]), ("fm", blk), win_fm[blk], s_fm[blk])
        for j in range(4):
            pb = j % 4
            if blk == 0 and j == 0:
                mms_split(P[pb][:], [(wsl[w_][:, kc * 512 + j * 128:kc * 512 + j * 128 + 128], u[:, kc, :]) for kc in range(8)],
                          [wsl_t[w_]], u_t, [P_t[pb]])
            else:
                mms(P[pb][:], [(wsl[w_][:, kc * 512 + j * 128:kc * 512 + j * 128 + 128], u[:, kc, :]) for kc in range(8)],
                    [wsl_t[w_]] + u_t, [P_t[pb]])
            evac(j, pb)

    def mixer(t):
        norm_mod(1, pre=True)
        tok0 = t * TT_
        fm_block(0, lambda j, pb: act(qT[:, j, :], P[pb][:], AF.Identity, [P_t[pb]], [qT_t[j]], scale=128.0 ** -0.5))
        def ev_k(j, pb):
            act(kT[:, j, :], P[pb][:], AF.Identity, [P_t[pb]], [kT_t])
            for bb in range(2):
                kb.op("dve", lambda e, bb=bb, j=j, pb=pb: e.reduce_sum(out=kms[:, j * 2 + bb:j * 2 + bb + 1],
                                                                      in_=P[pb][:, bb * 256:(bb + 1) * 256],
                                                                      axis=mybir.AxisListType.X), [P_t[pb]], [kms_t])
        if STOP < 2.1:
            return
        fm_block(1, ev_k)
        for j in range(4):
            ts(kmean[:, j, 2 * t:2 * t + 2], kms[:, j * 2:j * 2 + 2], 1.0 / 256.0, None, ALU.mult, None, [kms_t], [kmean_t])
        if STOP < 2.2:
            return
        kb.dma("sp", s_K[:, :, tok0:tok0 + TT_], kT[:], reads=[kT_t], writes=[sK_t[t]])
        if STOP < 2.3:
            return
        def blk_fm(blk, evac):
            return lambda: fm_block(blk, evac)

        def blk_lr():
            w_ = wslot()
            wload((wsl[w_][:, 0:128], wsl_t[w_]), ("lr",), win_lr, s_lr)
            mms(P[4][0:16, :], [(wsl[w_][:, kc * 16:kc * 16 + 16], u[:, kc, :]) for kc in range(8)], [wsl_t[w_]] + u_t, [P_t[4]])
            cp(glrT[:], P[4][0:16, :], [P_t[4]], [glrT_t], eng="act_copy")

        def blk_tm(blk, ncol):
            def f():
                w_ = wslot()
                wload((wsl[w_][:, 0:4096], wsl_t[w_]), ("tm", blk), win_tm[blk], s_tm[blk])
                for g in range(4):
                    pb = g
                    mms(P[pb][:, 0:ncol], [(u[:, kc, g * 128:(g + 1) * 128], wsl[w_][:, kc * 512:kc * 512 + ncol]) for kc in range(8)],
                        [wsl_t[w_]] + u_t, [P_t[pb]])
                    if blk == 0:
                        act(Vt[:, g, :], P[pb][:], AF.Identity, [P_t[pb]], [Vt_t])
                    elif blk == 1:
                        act(vtm[:, g, :], P[pb][:], AF.Identity, [P_t[pb]], [vtm_t])
                    else:
                        cp(ktm[:, g, :], P[pb][:, 0:256], [P_t[pb]], [ktm_t], eng="act_copy")
            return f

        blocks = [
            blk_fm(2, lambda j, pb: cp(gqk[:, j, :], P[pb][:], [P_t[pb]], [gqk_t[j]], eng="act_copy")),
            blk_fm(3, lambda j, pb: act(gogs[:, j, :], P[pb][:], AF.Silu, [P_t[pb]], [gogs_t[j]])),
            blk_fm(4, lambda j, pb: act(sga(j), P[pb][:], AF.Sigmoid, [P_t[pb]], [sga_t(j)])),
            blk_fm(5, lambda j, pb: act(sga(4 + j), P[pb][:], AF.Sigmoid, [P_t[pb]], [sga_t(4 + j)])),
            blk_fm(6, lambda j, pb: act(sgb(j), P[pb][:], AF.Sigmoid, [P_t[pb]], [sgb_t(j)])),
            blk_fm(7, lambda j, pb: act(sgb(4 + j), P[pb][:], AF.Sigmoid, [P_t[pb]], [sgb_t(4 + j)])),
            blk_lr, blk_tm(0, 512), blk_tm(1, 512), blk_tm(2, 256),
        ]
        items = [(hh, g) for hh in range(4) for g in range(4)]

        def gate1(it, k_):
            hh, g = items[it]
            cur = 2 * t + g // 2
            kb.op("dve", lambda e: e.memset(gsb2[k_][:], -1e30), (), [gsb2_t[k_]])
            if cur > 0:
                mms(P[6][:, k_ * 32:k_ * 32 + 32], [(qT[:, hh, g * 128:(g + 1) * 128], kmean[:, hh, :])], [qT_t[hh], kmean_t], [P_t[6]])
                cp(gsb2[k_][:, 0:cur], P[6][:, k_ * 32:k_ * 32 + cur], [P_t[6]], [gsb2_t[k_]])
            kb.op("dve", lambda e: e.max(out=m82[k_][:], in_=gsb2[k_][:]), [gsb2_t[k_]], [m82_t[k_]])
            ts(nmk2[k_][:], gsb2[k_][:], m82[k_][:, 2:3], None, ALU.is_ge, None, [gsb2_t[k_], m82_t[k_]], [nmk2_t[k_]])
            ts(nmk2[k_][:], nmk2[k_][:], -1.0, BIG, ALU.add, ALU.mult, [nmk2_t[k_]], [nmk2_t[k_]])
            kb.op("dve", lambda e: e.memset(nmk2[k_][:, cur:cur + 1], 0.0), (), [nmk2_t[k_]])
            if cur + 1 < 32:
                kb.op("dve", lambda e: e.memset(nmk2[k_][:, cur + 1:32], -BIG), (), [nmk2_t[k_]])

        def gate2(it, k_):
            hh, g = items[it]
            pb = 5 if hh % 2 == 0 else 7
            kb.group("pe", [lambda e: e.transpose(P[pb][0:32, g * 128:(g + 1) * 128], nmk2[k_][:], cst[:, IDN:IDN + 128])],
                     [nmk2_t[k_], cst_t], [P_t[pb]])
            if g == 3:
                cp(nmT[0:32, hh, :], P[pb][0:32, :], [P_t[pb]], [nmT_t[hh]])

        it = 0
        for bi, bf_ in enumerate(blocks):
            mine = []
            while it < len(items) and len(mine) < 2:
                mine.append(it)
                it += 1
            for k_, i_ in enumerate(mine):
                gate1(i_, k_)
            bf_()
            for k_, i_ in enumerate(mine):
                gate2(i_, k_)
        assert it == len(items)
        if STOP < 2.6:
            return
        kb.dma("sp", s_V[tok0:tok0 + TT_, :].rearrange("(g p) d -> p g d", p=128), Vt[:], reads=[Vt_t], writes=[sV_t[t]])

        if STOP < 3:
            return
        gla(t)
        if STOP < 4:
            return
        moba(t)
        if STOP < 5:
            return

        for hh in range(4):
            s_ = hh % 2
            act(sq[s_][:], gqk[:, hh, :], AF.Square, [gqk_t[hh]], [sq_t[s_]])
            mms(P[6][:], [(ones_bf, sq[s_][:])], [sq_t[s_], cbf_t], [P_t[6]])
            act(tmp[s_][:], P[6][:], AF.Ln, [P_t[6], small_t], [tmp_t[s_]], bias=small[:, 0:1], scale=1.0 / 128.0)
            act(tmp[s_][:], tmp[s_][:], AF.Exp, [tmp_t[s_]], [tmp_t[s_]], scale=-0.5)
            stt(tmp[s_][:], gqk[:, hh, :], glan_s[:, 0:1], tmp[s_][:], ALU.mult, ALU.mult, [gqk_t[hh], glan_t, tmp_t[s_]], [tmp_t[s_]])
            tt(oab(hh), tmp[s_][:], gogs[:, hh, :], ALU.mult, [tmp_t[s_], gogs_t[hh]], [oab_t(hh)])
        w_a = wslot()
        wload((wsl[w_a][:, 0:4096], wsl_t[w_a]), ("br", 0), wbr[0], s_br[0])
        w_b = wslot()
        wload((wsl[w_b][:, 0:4096], wsl_t[w_b]), ("br", 1), wbr[1], s_br[1])
        for c in range(8):
            pa, pb = c % 2, 2 + c % 2
            mms(P[pa][:], [(wsl[w_a][:, kc * 1024 + c * 128:kc * 1024 + c * 128 + 128], oab(kc)) for kc in range(4)],
                [wsl_t[w_a]] + [oab_t(k_) for k_ in range(4)], [P_t[pa]])
            mms(P[pb][:], [(wsl[w_b][:, kc * 1024 + c * 128:kc * 1024 + c * 128 + 128], obT[:, kc, :]) for kc in range(4)],
                [wsl_t[w_b]] + obT_t, [P_t[pb]])
            s_ = c % 2
            tt(tmp[s_][:], P[pa][:], sga(c), ALU.mult, [P_t[pa], sga_t(c)], [tmp_t[s_]])
            tt(sg[s_][:], P[pb][:], sgb(c), ALU.mult, [P_t[pb], sgb_t(c)], [sg_t[s_]])
            tt(merged(c), tmp[s_][:], sg[s_][:], ALU.add, [tmp_t[s_], sg_t[s_]], [merged_t(c)], eng="pool")
        for blk in range(2):
            w_ = wslot()
            wload((wsl[w_][:, 0:4096], wsl_t[w_]), ("out", blk), wout[blk], s_out[blk])
            for j in range(4):
                c = blk * 4 + j
                pd = 4 + c % 2
                mms(P[pd][:], [(wsl[w_][:, kc * 512 + j * 128:kc * 512 + j * 128 + 128], merged(kc)) for kc in range(8)],
                    [wsl_t[w_]] + [merged_t(k_) for k_ in range(8)], [P_t[pd]])
                stt(h[:, c, :], P[pd][:], vecs[:, V_HG[1] + c:V_HG[1] + c + 1], h[:, c, :], ALU.mult, ALU.add,
                    [P_t[pd], vecs_t, h_t[c]], [h_t[c]])
                if c >= 1:
                    stat_chunk(c - 1)
        stat_chunk(7)

    def gla(t):
        sc16 = -1.0 / GLA_TAU
        for g in range(4):
            mms(P[4][:, 0:256], [(glrT[:, g * 128:(g + 1) * 128], wlr_s[:])], [glrT_t, wlr_t], [P_t[4]])
            tt(la[:, g, :], P[4][:, 0:256], blrt[:], ALU.add, [P_t[4], blrt_t], [la_t[g]])
            act(la[:, g, :], la[:, g, :], AF.Exp, [la_t[g]], [la_t[g]], scale=-1.0)
            act(la[:, g, :], la[:, g, :], AF.Ln, [la_t[g], small_t], [la_t[g]], bias=small[:, 1:2], scale=1.0)
        for pr in range(2):
            for g in range(4):
                mms(P[5][:, g * 128:(g + 1) * 128], [(la[:, g, pr * 128:(pr + 1) * 128], cst[:, TRI:TRI + 128])],
                    [la_t[g], cst_t], [P_t[5]])
            act(e12[0][:], P[5][:], AF.Exp, [P_t[5]], [e12_t[0]], scale=sc16)
            act(e12[1][:], P[5][:], AF.Exp, [P_t[5]], [e12_t[1]], scale=-sc16)
            stt(qe[:, pr, :], gqk[:, pr, :], 64.0 ** -0.5, e12[0][:], ALU.mult, ALU.mult, [gqk_t[pr], e12_t[0]], [qe_t[pr]])
            tt(ki[:, pr, :], gqk[:, 2 + pr, :], e12[1][:], ALU.mult, [gqk_t[2 + pr], e12_t[1]], [ki_t[pr]])
            for g4 in range(4):
                cp(ebl[:, pr, g4:g4 + 1], e12[0][:, g4 * 128 + 127:g4 * 128 + 128], [e12_t[0]], [ebl_t[pr]])
        for g in range(4):
            mms(P[6][:, 0:256], [(cst[:, TRI:TRI + 128], la[:, g, :])], [la_t[g], cst_t], [P_t[6]])
            mms(P[7][:, 0:256], [(cst[:, LBK:LBK + 128], la[:, g, :])], [la_t[g], cst_t], [P_t[7]])
            cp(bcs[:], P[6][:, 0:256], [P_t[6]], [bcs_t])
            tt(bcs[:], P[7][:, 0:256], bcs[:], ALU.subtract, [P_t[7], bcs_t], [bcs_t])
            act(bcs[:], bcs[:], AF.Exp, [bcs_t], [bcs_t], scale=sc16)
            tt(kd[:, g, :], ktm[:, g, :], bcs[:], ALU.mult, [ktm_t, bcs_t], [kd_t[g]])
        for g in range(4):
            cols = slice(g * 128, (g + 1) * 128)
            for hp in range(2):
                hs = (2 * hp, 2 * hp + 1)
                pr = hp
                R = [(hh % 2) * 64 for hh in hs]
                for s_, hh in enumerate(hs):
                    r0 = R[s_]
                    mms(P[4 + s_][:, 0:128], [(ki[r0:r0 + 64, pr, cols], qe[r0:r0 + 64, pr, cols])], [ki_t[pr], qe_t[pr]], [P_t[4 + s_]])
                for s_, hh in enumerate(hs):
                    mms(P[2 + s_][:, 0:128], [(kd[:, g, pr * 128:(pr + 1) * 128], vtm[:, g, hh * 128:(hh + 1) * 128])],
                        [kd_t[g], vtm_t], [P_t[2 + s_]])
                for s_, hh in enumerate(hs):
                    tt(attn[s_][:], P[4 + s_][:, 0:128], cst[:, M2C:M2C + 128], ALU.mult, [P_t[4 + s_], cst_t], [attn_t[s_]])
                for s_, hh in enumerate(hs):
                    r0 = R[s_]
                    mms(P[6 + s_][:, 0:128], [(vtm[:, g, hh * 128:(hh + 1) * 128], attn[s_][:]),
                                               (Sb[r0:r0 + 64, hh, :], qe[r0:r0 + 64, pr, cols])],
                        [vtm_t, attn_t[s_], Sb_t[hh], qe_t[pr]], [P_t[6 + s_]])
                for s_, hh in enumerate(hs):
                    r0 = R[s_]
                    stt(Sm[r0:r0 + 64, hh, :], Sm[r0:r0 + 64, hh, :], ebl[r0:r0 + 64, pr, g:g + 1], P[2 + s_][r0:r0 + 64, 0:128],
                        ALU.mult, ALU.add, [Sm_t[hh], ebl_t[pr], P_t[2 + s_]], [Sm_t[hh]])
                for s_, hh in enumerate(hs):
                    r0 = R[s_]
                    cp(Sb[r0:r0 + 64, hh, :], Sm[r0:r0 + 64, hh, :], [Sm_t[hh]], [Sb_t[hh]], eng="act_copy")
                for s_, hh in enumerate(hs):
                    cp(gqk[:, hh, cols], P[6 + s_][:, 0:128], [P_t[6 + s_]], [gqk_t[hh]], eng="act_copy")

    def moba(t):
        LAG = 1
        for hp in range(2):
            heads = (2 * hp, 2 * hp + 1)
            kvslot = {}

            def load_kv(kt, hp=hp, kvslot=kvslot):
                if kt >= t or kt in kvslot:
                    return
                s_ = kvc[0] % NKV
                kvc[0] += 1
                kvslot[kt] = s_
                kb.dma("sp", kvK[s_][:], s_K[:, 2 * hp:2 * hp + 2, kt * TT_:(kt + 1) * TT_], reads=[sK_t[kt]], writes=[kvK_t[s_]])
                kb.dma("sp", kvV[s_][:], s_V[kt * TT_:(kt + 1) * TT_, hp * 256:(hp + 1) * 256].rearrange("(g p) d -> p g d", p=128),
                       reads=[sV_t[kt]], writes=[kvV_t[s_]])

            units = [(kt, cc) for kt in range(t + 1) for cc in range(4)]
            n = len(units)
            SB_ = [4, 5, 6, 7]

            def srcs(kt, kvslot=kvslot):
                if kt == t:
                    return (lambda hh, cc: kT[:, hh, cc * 128:(cc + 1) * 128],
                            lambda hh, cc: Vt[:, cc, hh * 128:(hh + 1) * 128], [kT_t, Vt_t])
                s_ = kvslot[kt]
                return (lambda hh, cc, s_=s_: kvK[s_][:, hh % 2, cc * 128:(cc + 1) * 128],
                        lambda hh, cc, s_=s_: kvV[s_][:, cc, (hh % 2) * 128:(hh % 2) * 128 + 128], [kvK_t[s_], kvV_t[s_]])

            def stage1(i):
                kt, cc = units[i]
                if cc == 0:
                    load_kv(kt)
                    load_kv(kt + 1)
                Ksrc, Vsrc, kv_reads = srcs(kt)
                jc = kt * 4 + cc
                blk = jc // 2
                rel = jc - 4 * t
                near = rel >= -1
                banks = [SB_[(2 * i + hi) % 4] for hi in range(2)]
                fns = []
                for hi, hh in enumerate(heads):
                    fns.append(lambda e, hi=hi, hh=hh: e.matmul(P[banks[hi]][:], Ksrc(hh, cc), qT[:, hh, :], start=True, stop=False))
                    fns.append(lambda e, hi=hi, hh=hh: e.matmul(P[banks[hi]][:], esel[:, blk * 128:(blk + 1) * 128], nmT[:, hh, :], start=False, stop=True))
                mm_raw(fns, kv_reads + [qT_t[heads[0]], qT_t[heads[1]], esel_t, nmT_t[heads[0]], nmT_t[heads[1]]],
                       [P_t[banks[0]], P_t[banks[1]]])
                for hi, hh in enumerate(heads):
                    p_ = (2 * i + hi) % 6
                    sp_ = banks[hi]
                    if near:
                        w0 = hh * 1024 + 384 - 128 * rel
                        tt(ssb[hi][:], P[sp_][:], base[:, w0:w0 + 512], ALU.add, [P_t[sp_], base_t], [ssb_t[hi]])
                        act(pt[p_][:], ssb[hi][:], AF.Exp, [ssb_t[hi]], [pt_t[p_]])
                    else:
                        act(pt[p_][:], P[sp_][:], AF.Exp, [P_t[sp_], b31_t], [pt_t[p_]], bias=b31s[:, hh:hh + 1], scale=1.0)

            def stage2(i):
                kt, cc = units[i]
                Ksrc, Vsrc, kv_reads = srcs(kt)
                first = (kt == 0 and cc == 0)
                last = (kt == t and cc == 3)
                ps_ = [(2 * i + hi) % 6 for hi in range(2)]
                fns = []
                for hi, hh in enumerate(heads):
                    fns.append(lambda e, hi=hi, hh=hh: e.matmul(P[hi][:], Vsrc(hh, cc), pt[ps_[hi]][:], start=first, stop=last))
                    fns.append(lambda e, hi=hi: e.matmul(P[2 + hi][:], ones_bf, pt[ps_[hi]][:], start=first, stop=last))
                mm_raw(fns, kv_reads + [pt_t[ps_[0]], pt_t[ps_[1]], cbf_t], [P_t[0], P_t[1], P_t[2], P_t[3]])

            for i in range(n + LAG):
                if i < n:
                    stage1(i)
                if i >= LAG:
                    stage2(i - LAG)
            for hi, hh in enumerate(heads):
                kb.op("dve", lambda e, hi=hi: e.reciprocal(out=den[:], in_=P[2 + hi][:]), [P_t[2 + hi]], [den_t])
                tt(obT[:, hh, :], P[hi][:], den[:], ALU.mult, [P_t[hi], den_t], [obT_t[hh]])

    kvc = [0]

    for t in range(NT):
        cur_t[0] = t
        tok0 = t * TT_
        for c in range(8):
            kb.dma("sp", h[:, c, :], xT[:, c, tok0:tok0 + TT_], writes=[h_t[c]])
        if STOP >= 1:
            ffn(0, 0)
        if STOP >= 2:
            mixer(t)
        if STOP >= 6:
            ffn(2, 1, pre=True)
        stats_finish()
        for c in range(8):
            stt(h[:, c, :], h[:, c, :], gn[:, 24 + c:24 + c + 1], rstd[:], ALU.mult, ALU.mult, [h_t[c], gn_t, rstd_t], [h_t[c]])
            kb.dma("sp", outT[:, c, tok0:tok0 + TT_], h[:, c, :], reads=[h_t[c]], is_output=True)

    kb.finish()
    print("sbuf bytes remaining/partition:", nc.sbuf_bytes_remaining, " instr counts:", kb.cnt)
    kb.close()
    es.close()
    return nc


def _rel_bucket_np(dist):
    n = np.maximum(dist, 0)
    nf = np.maximum(n, 1).astype(np.float32)
    large = 16 + (np.log(nf / np.float32(16)) / np.float32(math.log(128 / 16)) * np.float32(16)).astype(np.int32)
    large = np.minimum(large, 31)
    return np.where(n < 16, n, large)


def _host_inputs(inp, b, NT):
    S = NT * TT_
    f = np.float32
    x = np.asarray(inp["x"], f)[b, :S]
    m = {}
    m["xT"] = np.ascontiguousarray(x.T.reshape(8, 128, S).transpose(1, 0, 2))
    m["cT"] = np.ascontiguousarray(np.asarray(inp["c"], f)[b].reshape(8, 128).T)
    wa = np.asarray(inp["w_ada"], f)[0]
    m["w_ada"] = np.ascontiguousarray(wa.reshape(8, 128, 36, 256).transpose(2, 1, 0, 3).reshape(36, 128, 2048))
    m["b_ada"] = np.ascontiguousarray(np.asarray(inp["b_ada"], f)[0].reshape(72, 128).T)
    gs = [np.asarray(inp[k], f).reshape(-1) for k in ("norm_ff1", "norm_mix", "norm_ff2", "norm_final")]
    m["gains"] = np.ascontiguousarray(np.concatenate([g.reshape(8, 128).T for g in gs], axis=1))
    ffw = ((inp["w_ff1_gate"], inp["w_ff1_up"], inp["w_ff1_down"]), (inp["w_ff2_gate"], inp["w_ff2_up"], inp["w_ff2_down"]))
    for i in range(2):
        wg = np.asarray(ffw[i][0], f)[0]
        wu = np.asarray(ffw[i][1], f)[0]
        wd = np.asarray(ffw[i][2], f)[0]
        g4 = wg.reshape(8, 128, 11, 256).transpose(2, 1, 0, 3)
        u4 = wu.reshape(8, 128, 11, 256).transpose(2, 1, 0, 3)
        m["wgu%d" % i] = np.ascontiguousarray(np.stack([g4, u4], axis=2).reshape(11, 128, 4096))
        m["wdn%d" % i] = np.ascontiguousarray(wd.reshape(NFC, 128, 8, 128).transpose(2, 1, 0, 3).reshape(8, 128, NFC * 128))
    wi = np.asarray(inp["w_in"], f)[0]

    def cols(a0, n):
        return wi[:, a0:a0 + n]

    def blk512(wc):
        return wc.reshape(8, 128, 512).transpose(1, 0, 2).reshape(128, 4096)

    fm = [cols(OFF["mq"], 512), cols(OFF["mk"], 512), np.concatenate([cols(OFF["gq"], 256), cols(OFF["gk"], 256)], axis=1),
          cols(OFF["gog"], 512), cols(OFF["ga"], 512), cols(OFF["ga"] + 512, 512), cols(OFF["gb"], 512), cols(OFF["gb"] + 512, 512)]
    m["win_fm"] = np.ascontiguousarray(np.stack([blk512(w_) for w_ in fm]))
    gk_pad = np.concatenate([cols(OFF["gk"], 256), np.zeros((1024, 256), f)], axis=1)
    m["win_tm"] = np.ascontiguousarray(np.stack([blk512(cols(OFF["mv"], 512)), blk512(cols(OFF["gv"], 512)), blk512(gk_pad)]))
    m["win_lr"] = np.ascontiguousarray(cols(OFF["glr"], 16).reshape(8, 128, 16).transpose(1, 0, 2).reshape(128, 128))
    m["wlr"] = np.ascontiguousarray(np.asarray(inp["w_gla_lr"], f)[0])
    blr = np.asarray(inp["b_gla_lr"], f)[0]
    m["blr_fm"] = np.ascontiguousarray(blr.reshape(2, 128).T)
    m["blr_tm"] = np.ascontiguousarray(np.broadcast_to(blr[None, :], (128, 256)))
    m["glan"] = np.ascontiguousarray(np.asarray(inp["gla_norm"], f)[0].reshape(128, 1))
    brg = np.asarray(inp["w_br_gla"], f)[0]
    brm = np.asarray(inp["w_br_moba"], f)[0]
    m["wbr"] = np.ascontiguousarray(np.stack([w_.reshape(4, 128, 1024).transpose(1, 0, 2).reshape(128, 4096) for w_ in (brg, brm)]))
    wo = np.asarray(inp["w_out"], f)[0]
    m["wout"] = np.ascontiguousarray(wo.reshape(8, 128, 2, 512).transpose(2, 1, 0, 3).reshape(2, 128, 4096))
    rb = np.asarray(inp["rel_bias"], f)
    kk = np.arange(128)[:, None]
    jj = np.arange(1024)[None, :]
    dist = jj - 384 - kk
    bidx = _rel_bucket_np(dist)
    basev = rb[:, bidx]
    basev = np.where(dist[None] >= 0, basev, f(-BIG))
    m["relbase"] = np.ascontiguousarray(basev.transpose(1, 0, 2).reshape(128, 4096).astype(f))
    m["b31"] = np.ascontiguousarray(np.broadcast_to(rb[:, 31][None, :], (128, 4)).astype(f))
    ii = np.arange(128)
    ident = np.eye(128, dtype=f)
    tri = (ii[:, None] <= ii[None, :]).astype(f)
    lblk = np.ones((128, 128), f)
    m["consts"] = np.ascontiguousarray(np.concatenate([ident, tri, lblk, tri, np.ones((128, 128), f)], axis=1))
    es_ = np.zeros((128, 32, 128), f)
    for n in range(32):
        es_[n, n, :] = 1.0
    m["esel"] = np.ascontiguousarray(es_.reshape(128, 32 * 128))
    return m


_PROG_CACHE = {}


def run(inputs, NT=NT_FULL, n_cores=N_CORES):
    if NT not in _PROG_CACHE:
        _PROG_CACHE[NT] = build_program(NT)
    nc = _PROG_CACHE[NT]
    in_maps = [_host_inputs(inputs, b, NT) for b in range(n_cores)]
    res = run_bass_kernel_spmd(nc, in_maps, core_ids=list(range(n_cores)))
    S = NT * TT_
    outs = []
    for b in range(n_cores):
        oT = np.asarray(res.results[b]["outT"])
        outs.append(oT.transpose(2, 1, 0).reshape(S, D))
    return np.stack(outs).astype(np.float32)


def kernel(**inputs):
    return run(inputs)
```

```python
import numpy as np
import math
import concourse.bass as bass
import concourse.mybir as mybir
from concourse.bass_utils import run_bass_kernel_spmd

F32 = mybir.dt.float32
BF16 = mybir.dt.bfloat16
AF = mybir.ActivationFunctionType
ALU = mybir.AluOpType

D = 1024
SEQ = 8192
NB = 4
F = 2816
NFC = 22
TT_ = 512
NT_FULL = SEQ // TT_
EPS = 1e-6
BIG = 30000.0
GLA_TAU = 16.0
N_CORES = 4

OFF = dict(gq=0, gk=256, gv=512, glr=1024, gog=1040, mq=1552, mk=2064, mv=2576, ga=3088, gb=4112)


class T:
    __slots__ = ("w", "r", "name", "ex")

    def __init__(self, name="", ex=False):
        self.w = None
        self.r = {}
        self.name = name
        self.ex = ex


class KB:
    def __init__(self, nc, n_dma_sems=24, same_engine_sync=True):
        self.nc = nc
        self.eng = {"pe": nc.tensor, "act": nc.scalar, "dve": nc.vector, "pool": nc.gpsimd, "sp": nc.sync}
        self.same = same_engine_sync
        self.sem = {}
        self.cnt = {}
        self._cms = []
        for e in self.eng:
            cm = nc.semaphore("prog_" + e)
            self._cms.append(cm)
            self.sem[e] = cm.__enter__()
            self.cnt[e] = 0
        self.dsem = []
        self.dtot = []
        for i in range(n_dma_sems + 8):
            cm = nc.semaphore("dma_%d" % i)
            self._cms.append(cm)
            self.dsem.append(cm.__enter__())
            self.dtot.append(0)
        self.dpool = {"sp": list(range(n_dma_sems)), "pool": list(range(n_dma_sems, n_dma_sems + 8))}
        self.dnext = {"sp": 0, "pool": 0}
        self.seen = {e: {} for e in self.eng}
        self.out_tokens = []

    def close(self):
        for cm in reversed(self._cms):
            cm.__exit__(None, None, None)

    def _wait(self, e, tok):
        sem, val, own = tok
        if own == e and (e == "pe" or not self.same):
            return
        key = id(sem)
        if self.seen[e].get(key, 0) >= val:
            return
        self.eng[e].wait_ge(sem, val)
        self.seen[e][key] = val

    def _deps(self, e, reads, writes):
        toks = []
        for t in reads:
            if t.w is not None:
                toks.append(t.w)
            if t.ex:
                toks.extend(tk for tk in t.r.values() if tk[2] != e)
        for t in writes:
            if t.w is not None:
                toks.append(t.w)
            toks.extend(t.r.values())
        best = {}
        for tok in toks:
            k = id(tok[0])
            if k not in best or best[k][1] < tok[1]:
                best[k] = tok
        for tok in best.values():
            self._wait(e, tok)

    def _commit(self, tok, reads, writes):
        k = id(tok[0])
        for t in reads:
            t.r[k] = tok
        for t in writes:
            t.w = tok
            t.r = {}

    def op(self, e, fn, reads=(), writes=()):
        self._deps(e, reads, writes)
        self.cnt[e] += 1
        tok = (self.sem[e], self.cnt[e], e)
        fn(self.eng[e]).then_inc(self.sem[e], 1)
        self._commit(tok, reads, writes)

    def group(self, e, fns, reads=(), writes=()):
        self._deps(e, reads, writes)
        self.cnt[e] += 1
        tok = (self.sem[e], self.cnt[e], e)
        for fn in fns[:-1]:
            fn(self.eng[e])
        fns[-1](self.eng[e]).then_inc(self.sem[e], 1)
        self._commit(tok, reads, writes)

    def dma(self, q, out_ap, in_ap, reads=(), writes=(), is_output=False):
        pl = self.dpool[q]
        i = pl[self.dnext[q]]
        self.dnext[q] = (self.dnext[q] + 1) % len(pl)
        sem = self.dsem[i]
        if self.dtot[i] > 0:
            self._wait(q, (sem, self.dtot[i], "dma"))
        self._deps(q, reads, writes)
        self.dtot[i] += 16
        tok = (sem, self.dtot[i], "dma")
        self.eng[q].dma_start(out=out_ap, in_=in_ap).then_inc(sem, 16)
        self._commit(tok, reads, writes)
        if is_output:
            self.out_tokens.append(tok)

    def finish(self):
        for tok in self.out_tokens:
            self._wait("sp", tok)
        for i, sem in enumerate(self.dsem):
            if self.dtot[i] > 0:
                self._wait("sp", (sem, self.dtot[i], "dma"))


DEBUG_STOP = 99


def build_program(NT):
    STOP = DEBUG_STOP
    nc = bass.Bass("TRN2", target_bir_lowering=False)
    S = NT * TT_
    NBLK = S // 256

    def din(name, shape, dt=F32):
        return nc.dram_tensor(name, list(shape), dt, kind="ExternalInput").ap()

    def dscr(name, shape, dt=BF16):
        return nc.dram_tensor(name, list(shape), dt, kind="Internal").ap()

    xT = din("xT", [128, 8, S])
    cT = din("cT", [128, 8])
    w_ada = din("w_ada", [36, 128, 8 * 256])
    b_ada = din("b_ada", [128, 72])
    gains = din("gains", [128, 32])
    wgu = [din("wgu%d" % i, [11, 128, 4096]) for i in range(2)]
    wdn = [din("wdn%d" % i, [8, 128, NFC * 128]) for i in range(2)]
    win_fm = din("win_fm", [8, 128, 4096])
    win_tm = din("win_tm", [3, 128, 4096])
    win_lr = din("win_lr", [128, 8 * 16])
    wlr = din("wlr", [16, 256])
    blr_fm = din("blr_fm", [128, 2])
    blr_tm = din("blr_tm", [128, 256])
    glan = din("glan", [128, 1])
    wbr = din("wbr", [2, 128, 4096])
    wout = din("wout", [2, 128, 4096])
    relbase = din("relbase", [128, 4 * 1024])
    b31 = din("b31", [128, 4])
    consts = din("consts", [128, 5 * 128])
    esel_in = din("esel", [128, 32 * 128])
    outT = nc.dram_tensor("outT", [128, 8, S], F32, kind="ExternalOutput").ap()

    s_wgu = [dscr("s_wgu%d" % i, [11, 128, 4096]) for i in range(2)]
    s_wdn = [dscr("s_wdn%d" % i, [8, 128, NFC * 128]) for i in range(2)]
    s_fm = dscr("s_fm", [8, 128, 4096])
    s_tm = dscr("s_tm", [3, 128, 4096])
    s_lr = dscr("s_lr", [128, 128])
    s_br = dscr("s_br", [2, 128, 4096])
    s_out = dscr("s_out", [2, 128, 4096])
    s_K = dscr("s_K", [128, 4, S])
    s_V = dscr("s_V", [S, 512])

    import contextlib
    es = contextlib.ExitStack()
    kb = KB(nc)

    def sb(name, shape, dt=F32):
        return es.enter_context(nc.sbuf_tensor(name, list(shape), dt))

    def ps(name):
        return es.enter_context(nc.psum_tensor(name, [128, 512], F32))

    hb = [sb("h%d" % i, [128, 8, 512]) for i in range(2)]; hb_t = [[T("h%d_%d" % (i, c)) for c in range(8)] for i in range(2)]
    h, h_t = hb[1], hb_t[1]
    u = sb("u", [128, 8, 512], BF16); u_t = [T("u%d" % i) for i in range(8)]
    a = sb("a", [128, NFC, 512], BF16); a_t = [T("a%d" % i) for i in range(NFC)]
    NW = 3
    wsl = [sb("wsl%d" % i, [128, 4096], BF16) for i in range(NW)]; wsl_t = [T("wsl%d" % i) for i in range(NW)]
    rstd = sb("rstd", [128, 512]); rstd_t = T("rstd")
    sq = [sb("sq%d" % i, [128, 512], BF16) for i in range(2)]; sq_t = [T() for _ in range(2)]
    sg = [sb("sg%d" % i, [128, 512]) for i in range(2)]; sg_t = [T() for _ in range(2)]
    qT = sb("qT", [128, 4, 512], BF16); qT_t = [T() for _ in range(4)]
    kT = sb("kT", [128, 4, 512], BF16); kT_t = T("kT")
    Vt = sb("Vt", [128, 4, 512], BF16); Vt_t = T("Vt")
    gqk = sb("gqk", [128, 4, 512]); gqk_t = [T() for _ in range(4)]
    glrT = sb("glrT", [16, 512], BF16); glrT_t = T("glrT")
    ktm = sb("ktm", [128, 4, 256]); ktm_t = T("ktm")
    vtm = sb("vtm", [128, 4, 512], BF16); vtm_t = T("vtm")
    kd = sb("kd", [128, 4, 256], BF16); kd_t = [T() for _ in range(4)]
    la = sb("la", [128, 4, 256]); la_t = [T() for _ in range(4)]
    bcs = sb("bcs", [128, 256]); bcs_t = T("bcs")
    qe = sb("qe", [128, 2, 512], BF16); qe_t = [T() for _ in range(2)]
    ki = sb("ki", [128, 2, 512], BF16); ki_t = [T() for _ in range(2)]
    ebl = sb("ebl", [128, 2, 8]); ebl_t = [T() for _ in range(2)]
    attn = [sb("attn%d" % i, [128, 128], BF16) for i in range(2)]; attn_t = [T() for _ in range(2)]
    Sm = sb("Sm", [128, 4, 128]); Sm_t = [T() for _ in range(4)]
    Sb = sb("Sb", [128, 4, 128], BF16); Sb_t = [T() for _ in range(4)]
    obT = sb("obT", [128, 4, 512], BF16); obT_t = [T() for _ in range(4)]
    pt = [sb("pt%d" % i, [128, 512], BF16) for i in range(6)]; pt_t = [T() for _ in range(6)]
    ssb = [sb("ssb%d" % i, [128, 512]) for i in range(2)]; ssb_t = [T() for _ in range(2)]
    e12, e12_t = ssb, ssb_t
    tmp, tmp_t = ssb, ssb_t
    NKV = 3
    kvK = [sb("kvK%d" % i, [128, 2, 512], BF16) for i in range(NKV)]; kvK_t = [T() for _ in range(NKV)]
    kvV = [sb("kvV%d" % i, [128, 4, 256], BF16) for i in range(NKV)]; kvV_t = [T() for _ in range(NKV)]
    base = sb("base", [128, 4 * 1024]); base_t = T("base")
    b31s = sb("b31s", [128, 4]); b31_t = T("b31")
    cst = sb("cst", [128, 5 * 128]); cst_t = T("cst")
    cbf = sb("cbf", [128, 5 * 128], BF16); cbf_t = T("cbf")
    esel = sb("eselb", [128, 32 * 128], BF16); esel_t = T("esel")
    kmean = sb("kmean", [128, 4, 32], BF16); kmean_t = T("kmean")
    kms = sb("kms", [128, 8]); kms_t = T("kms")
    gsb2 = [sb("gsb%d" % i, [128, 32]) for i in range(2)]; gsb2_t = [T() for _ in range(2)]
    m82 = [sb("m8_%d" % i, [128, 8]) for i in range(2)]; m82_t = [T() for _ in range(2)]
    nmk2 = [sb("nmk%d" % i, [128, 32]) for i in range(2)]; nmk2_t = [T() for _ in range(2)]
    nmT = sb("nmT", [128, 4, 512], BF16); nmT_t = [T() for _ in range(4)]
    mod = sb("mod", [128, 72]); mod_t = T("mod")
    vecs = sb("vecs", [128, 64]); vecs_t = T("vecs")
    gn = sb("gn", [128, 32]); gn_t = T("gn")
    cact = sb("cact", [128, 8]); cact_t = T("cact")
    small = sb("small", [128, 16]); small_t = T("small")
    wlr_s = sb("wlr_s", [16, 256], BF16); wlr_t = T("wlr")
    blrf = sb("blrf", [128, 2]); blrf_t = T("blrf")
    blrt = sb("blrt", [128, 256]); blrt_t = T("blrt")
    glan_s = sb("glan_s", [128, 1]); glan_t = T("glan")
    den, den_t = rstd, rstd_t
    rowb = [sb("rowb%d" % i, [1, 256]) for i in range(2)]; rowb_t = [T() for _ in range(2)]

    P = [ps("ps%d" % i) for i in range(8)]; P_t = [T("ps%d" % i, ex=True) for i in range(8)]

    sga = lambda c: a[:, c, :]; sga_t = lambda c: a_t[c]
    sgb = lambda c: a[:, 8 + c, :]; sgb_t = lambda c: a_t[8 + c]
    oab = lambda hh: a[:, 16 + hh, :]; oab_t = lambda hh: a_t[16 + hh]
    gog = lambda hh: a[:, 20 + hh // 2, (hh % 2) * 256:(hh % 2) * 256 + 256]
    merged = lambda c: u[:, c, :]; merged_t = lambda c: u_t[c]
    gogs = sb("gogs", [128, 4, 512], BF16); gogs_t = [T() for _ in range(4)]

    IDN, TRI, LBK, M2C, ONE = 0, 128, 256, 384, 512
    V_GSC = [0, 8, 16]
    V_HG = [24, 32, 40]
    V_NEGBLR = 48
    M_SH = [0, 24, 48]
    M_SC = [8, 32, 56]
    M_G = [16, 40, 64]

    def act(out, in_, func, reads, writes, bias=None, scale=None):
        kw = {}
        if bias is not None:
            kw["bias"] = bias
        if scale is not None:
            kw["scale"] = scale
        kb.op("act", lambda e: e.activation(out=out, in_=in_, func=func, **kw), reads, writes)

    def tt(out, in0, in1, op, reads, writes, eng="dve"):
        kb.op(eng, lambda e: e.tensor_tensor(out=out, in0=in0, in1=in1, op=op), reads, writes)

    def ts(out, in0, s1, s2, op0, op1, reads, writes, eng="dve"):
        if s2 is None:
            kb.op(eng, lambda e: e.tensor_scalar(out=out, in0=in0, scalar1=s1, scalar2=None, op0=op0), reads, writes)
        else:
            kb.op(eng, lambda e: e.tensor_scalar(out=out, in0=in0, scalar1=s1, scalar2=s2, op0=op0, op1=op1), reads, writes)

    def stt(out, in0, scalar, in1, op0, op1, reads, writes, eng="dve"):
        kb.op(eng, lambda e: e.scalar_tensor_tensor(out=out, in0=in0, scalar=scalar, in1=in1, op0=op0, op1=op1), reads, writes)

    def cp(out, in_, reads, writes, eng="dve"):
        if eng == "act_copy":
            act(out, in_, AF.Identity, reads, writes)
        else:
            kb.op(eng, lambda e: e.tensor_copy(out=out, in_=in_), reads, writes)

    def mms_split(out, items, common_reads, per_reads, writes):
        n = len(items)
        for i, (l, r) in enumerate(items):
            kb.group("pe", [lambda e, l=l, r=r, i=i: e.matmul(out, l, r, start=(i == 0), stop=(i == n - 1))],
                     list(common_reads) + [per_reads[i]], writes)

    def mms(out, items, reads, writes):
        n = len(items)
        fns = []
        for i, (l, r) in enumerate(items):
            fns.append(lambda e, l=l, r=r, i=i: e.matmul(out, l, r, start=(i == 0), stop=(i == n - 1)))
        kb.group("pe", fns, reads, writes)

    def mm_raw(fns, reads, writes):
        kb.group("pe", fns, reads, writes)

    dq = "sp"
    kb.dma(dq, cst[:], consts, writes=[cst_t])
    kb.dma(dq, base[:], relbase, writes=[base_t])
    kb.dma(dq, b31s[:], b31, writes=[b31_t])
    kb.dma(dq, gn[:], gains, writes=[gn_t])
    kb.dma(dq, cact[:], cT, writes=[cact_t])
    kb.dma(dq, mod[:], b_ada, writes=[mod_t])
    kb.dma(dq, blrf[:], blr_fm, writes=[blrf_t])
    kb.dma(dq, blrt[:], blr_tm, writes=[blrt_t])
    kb.dma(dq, glan_s[:], glan, writes=[glan_t])
    kb.dma("pool", wlr_s[:], wlr, writes=[wlr_t])
    kb.dma("pool", esel[:], esel_in, writes=[esel_t])
    cp(cbf[:], cst[:], [cst_t], [cbf_t])
    kb.op("dve", lambda e: e.memset(small[:, 0:1], EPS), (), [small_t])
    kb.op("dve", lambda e: e.memset(small[:, 1:2], 1.0), (), [small_t])
    kb.op("dve", lambda e: e.memset(small[:, 2:3], 0.0), (), [small_t])
    kb.op("dve", lambda e: e.memset(kmean[:], 0.0), (), [kmean_t])
    for hh in range(4):
        kb.op("dve", lambda e, hh=hh: e.memset(nmT[:, hh, :], 0.0), (), [nmT_t[hh]])
    for hh in range(4):
        kb.op("dve", lambda e, hh=hh: e.memset(Sm[:, hh, :], 0.0), (), [Sm_t[hh]])
        kb.op("dve", lambda e, hh=hh: e.memset(Sb[:, hh, :], 0.0), (), [Sb_t[hh]])

    scr_t = {}

    cur_t = [0]

    def wload(dst_ap, key, src_f32, scr_ap):
        if cur_t[0] == 0:
            kb.dma("pool", dst_ap[0], src_f32, writes=[dst_ap[1]])
            t_ = T(str(key))
            scr_t[key] = t_
            kb.dma("pool", scr_ap, src_f32, writes=[t_])
        else:
            kb.dma("pool", dst_ap[0], scr_ap, reads=[scr_t[key]], writes=[dst_ap[1]])

    def load_x(t_):
        for c in range(8):
            kb.dma("sp", hb[t_ % 2][:, c, :], xT[:, c, t_ * TT_:(t_ + 1) * TT_], writes=[hb_t[t_ % 2][c]])

    load_x(0)
    act(cact[:], cact[:], AF.Silu, [cact_t], [cact_t])
    for pc in range(36):
        s_ = pc % 2
        wt_ = h_t[4 * s_:4 * s_ + 4]
        kb.dma("sp", h[:, 4 * s_:4 * s_ + 4, :], w_ada[pc].rearrange("p (c n) -> p c n", c=4), writes=wt_)
        mms(P[0][0:1, s_ * 256:(s_ + 1) * 256],
            [(cact[:, kc:kc + 1], h[:, 4 * s_ + kc // 2, (kc % 2) * 256:(kc % 2) * 256 + 256]) for kc in range(8)],
            wt_ + [cact_t], [P_t[0]])
        cp(rowb[s_][:], P[0][0:1, s_ * 256:(s_ + 1) * 256], [P_t[0]], [rowb_t[s_]], eng="act_copy")
        for j in range(2):
            col = pc * 2 + j
            mms(P[1][:, col:col + 1], [(rowb[s_][0:1, j * 128:(j + 1) * 128], small[0:1, 1:2])], [rowb_t[s_], small_t], [P_t[1]])
    tt(mod[:], mod[:], P[1][:, 0:72], ALU.add, [mod_t, P_t[1]], [mod_t])
    for i in range(3):
        stt(vecs[:, V_GSC[i]:V_GSC[i] + 8], mod[:, M_SC[i]:M_SC[i] + 8], 1.0, gn[:, i * 8:i * 8 + 8], ALU.add, ALU.mult,
            [mod_t, gn_t], [vecs_t])
    ts(vecs[:, V_HG[0]:V_HG[0] + 8], mod[:, M_G[0]:M_G[0] + 8], 0.5, None, ALU.mult, None, [mod_t], [vecs_t])
    cp(vecs[:, V_HG[1]:V_HG[1] + 8], mod[:, M_G[1]:M_G[1] + 8], [mod_t], [vecs_t])
    ts(vecs[:, V_HG[2]:V_HG[2] + 8], mod[:, M_G[2]:M_G[2] + 8], 0.5, None, ALU.mult, None, [mod_t], [vecs_t])
    ts(vecs[:, V_NEGBLR:V_NEGBLR + 2], blrf[:], -1.0, None, ALU.mult, None, [blrf_t], [vecs_t])

    wcur = [0]
    WQ = "pool"

    def wslot():
        i = wcur[0]
        wcur[0] = (i + 1) % NW
        return i

    ones_bf = cbf[:, ONE:ONE + 128]

    def stat_chunk(c):
        s_ = c % 2
        act(sq[s_][:], h[:, c, :], AF.Square, [h_t[c]], [sq_t[s_]])
        kb.group("pe", [lambda e: e.matmul(P[7][:], ones_bf, sq[s_][:], start=(c == 0), stop=(c == 7))],
                 [sq_t[s_], cbf_t], [P_t[7]])

    def stats_finish():
        act(rstd[:], P[7][:], AF.Ln, [P_t[7], small_t], [rstd_t], bias=small[:, 0:1], scale=1.0 / D)
        act(rstd[:], rstd[:], AF.Exp, [rstd_t], [rstd_t], scale=-0.5)

    def norm_mod(i, pre=False):
        if not pre:
            for c in range(8):
                stat_chunk(c)
        stats_finish()
        for c in range(8):
            s_ = c % 2
            stt(tmp[s_][:], h[:, c, :], vecs[:, V_GSC[i] + c:V_GSC[i] + c + 1], rstd[:], ALU.mult, ALU.mult,
                [h_t[c], vecs_t, rstd_t], [tmp_t[s_]])
            act(u[:, c, :], tmp[s_][:], AF.Identity, [tmp_t[s_], mod_t], [u_t[c]],
                bias=mod[:, M_SH[i] + c:M_SH[i] + c + 1], scale=1.0)

    def ffn(i, which, pre=False):
        norm_mod(i, pre)
        for blk in range(11):
            w_ = wslot()
            wload((wsl[w_][:, 0:4096], wsl_t[w_]), ("wgu", which, blk), wgu[which][blk], s_wgu[which][blk])
            for j in range(2):
                fc = blk * 2 + j
                pg, pu = fc % 2, 2 + fc % 2
                if fc == 0:
                    mms_split(P[pg][:], [(wsl[w_][:, kc * 256 + j * 128:kc * 256 + j * 128 + 128], u[:, kc, :]) for kc in range(8)],
                              [wsl_t[w_]], u_t, [P_t[pg]])
                else:
                    mms(P[pg][:], [(wsl[w_][:, kc * 256 + j * 128:kc * 256 + j * 128 + 128], u[:, kc, :]) for kc in range(8)],
                        [wsl_t[w_]] + u_t, [P_t[pg]])
                mms(P[pu][:], [(wsl[w_][:, 2048 + kc * 256 + j * 128:2048 + kc * 256 + j * 128 + 128], u[:, kc, :]) for kc in range(8)],
                    [wsl_t[w_]] + u_t, [P_t[pu]])
                act(sg[fc % 2][:], P[pg][:], AF.Silu, [P_t[pg]], [sg_t[fc % 2]])
                tt(a[:, fc, :], sg[fc % 2][:], P[pu][:], ALU.mult, [sg_t[fc % 2], P_t[pu]], [a_t[fc]])
        for c in range(8):
            w_ = wslot()
            wload((wsl[w_][:, 0:NFC * 128], wsl_t[w_]), ("wdn", which, c), wdn[which][c], s_wdn[which][c])
            pd = 4 + c % 2
            mms(P[pd][:], [(wsl[w_][:, fc * 128:fc * 128 + 128], a[:, fc, :]) for fc in range(NFC)],
                [wsl_t[w_]] + a_t, [P_t[pd]])
            stt(h[:, c, :], P[pd][:], vecs[:, V_HG[i] + c:V_HG[i] + c + 1], h[:, c, :], ALU.mult, ALU.add,
                [P_t[pd], vecs_t, h_t[c]], [h_t[c]])
            if c >= 1:
                stat_chunk(c - 1)
        stat_chunk(7)

    sK_t = [T("sK%d" % t) for t in range(NT)]
    sV_t = [T("sV%d" % t) for t in range(NT)]
    rr = [0]

    def nextp(lst):
        rr[0] += 1
        return lst[rr[0] % len(lst)]

    def fm_block(blk, evac):
        w_ = wslot()
        wload((wsl[w_][:, 0:4096], wsl_t[w_]), ("fm", blk), win_fm[blk], s_fm[blk])
        for j in range(4):
            pb = j % 4
            if blk == 0 and j == 0:
                mms_split(P[pb][:], [(wsl[w_][:, kc * 512 + j * 128:kc * 512 + j * 128 + 128], u[:, kc, :]) for kc in range(8)],
                          [wsl_t[w_]], u_t, [P_t[pb]])
            else:
                mms(P[pb][:], [(wsl[w_][:, kc * 512 + j * 128:kc * 512 + j * 128 + 128], u[:, kc, :]) for kc in range(8)],
                    [wsl_t[w_]] + u_t, [P_t[pb]])
            evac(j, pb)

    def mixer(t):
        norm_mod(1, pre=True)
        tok0 = t * TT_
        fm_block(0, lambda j, pb: act(qT[:, j, :], P[pb][:], AF.Identity, [P_t[pb]], [qT_t[j]], scale=128.0 ** -0.5))
        def ev_k(j, pb):
            act(kT[:, j, :], P[pb][:], AF.Identity, [P_t[pb]], [kT_t])
            for bb in range(2):
                kb.op("dve", lambda e, bb=bb, j=j, pb=pb: e.reduce_sum(out=kms[:, j * 2 + bb:j * 2 + bb + 1],
                                                                      in_=P[pb][:, bb * 256:(bb + 1) * 256],
                                                                      axis=mybir.AxisListType.X), [P_t[pb]], [kms_t])
        if STOP < 2.1:
            return
        fm_block(1, ev_k)
        for j in range(4):
            ts(kmean[:, j, 2 * t:2 * t + 2], kms[:, j * 2:j * 2 + 2], 1.0 / 256.0, None, ALU.mult, None, [kms_t], [kmean_t])
        if STOP < 2.2:
            return
        kb.dma("sp", s_K[:, :, tok0:tok0 + TT_], kT[:], reads=[kT_t], writes=[sK_t[t]])
        if STOP < 2.3:
            return
        def blk_fm(blk, evac):
            return lambda: fm_block(blk, evac)

        def blk_lr():
            w_ = wslot()
            wload((wsl[w_][:, 0:128], wsl_t[w_]), ("lr",), win_lr, s_lr)
            mms(P[4][0:16, :], [(wsl[w_][:, kc * 16:kc * 16 + 16], u[:, kc, :]) for kc in range(8)], [wsl_t[w_]] + u_t, [P_t[4]])
            cp(glrT[:], P[4][0:16, :], [P_t[4]], [glrT_t], eng="act_copy")

        def blk_tm(blk, ncol):
            def f():
                w_ = wslot()
                wload((wsl[w_][:, 0:4096], wsl_t[w_]), ("tm", blk), win_tm[blk], s_tm[blk])
                for g in range(4):
                    pb = g
                    mms(P[pb][:, 0:ncol], [(u[:, kc, g * 128:(g + 1) * 128], wsl[w_][:, kc * 512:kc * 512 + ncol]) for kc in range(8)],
                        [wsl_t[w_]] + u_t, [P_t[pb]])
                    if blk == 0:
                        act(Vt[:, g, :], P[pb][:], AF.Identity, [P_t[pb]], [Vt_t])
                    elif blk == 1:
                        act(vtm[:, g, :], P[pb][:], AF.Identity, [P_t[pb]], [vtm_t])
                    else:
                        cp(ktm[:, g, :], P[pb][:, 0:256], [P_t[pb]], [ktm_t], eng="act_copy")
            return f

        blocks = [
            blk_fm(2, lambda j, pb: cp(gqk[:, j, :], P[pb][:], [P_t[pb]], [gqk_t[j]], eng="act_copy")),
            blk_fm(3, lambda j, pb: act(gogs[:, j, :], P[pb][:], AF.Silu, [P_t[pb]], [gogs_t[j]])),
            blk_fm(4, lambda j, pb: act(sga(j), P[pb][:], AF.Sigmoid, [P_t[pb]], [sga_t(j)])),
            blk_fm(5, lambda j, pb: act(sga(4 + j), P[pb][:], AF.Sigmoid, [P_t[pb]], [sga_t(4 + j)])),
            blk_fm(6, lambda j, pb: act(sgb(j), P[pb][:], AF.Sigmoid, [P_t[pb]], [sgb_t(j)])),
            blk_fm(7, lambda j, pb: act(sgb(4 + j), P[pb][:], AF.Sigmoid, [P_t[pb]], [sgb_t(4 + j)])),
            blk_lr, blk_tm(0, 512), blk_tm(1, 512), blk_tm(2, 256),
        ]
        items = [(hh, g) for hh in range(4) for g in range(4)]

        def gate1(it, k_):
            hh, g = items[it]
            cur = 2 * t + g // 2
            kb.op("dve", lambda e: e.memset(gsb2[k_][:], -1e30), (), [gsb2_t[k_]])
            if cur > 0:
                mms(P[6][:, k_ * 32:k_ * 32 + 32], [(qT[:, hh, g * 128:(g + 1) * 128], kmean[:, hh, :])], [qT_t[hh], kmean_t], [P_t[6]])
                cp(gsb2[k_][:, 0:cur], P[6][:, k_ * 32:k_ * 32 + cur], [P_t[6]], [gsb2_t[k_]])
            kb.op("dve", lambda e: e.max(out=m82[k_][:], in_=gsb2[k_][:]), [gsb2_t[k_]], [m82_t[k_]])
            ts(nmk2[k_][:], gsb2[k_][:], m82[k_][:, 2:3], None, ALU.is_ge, None, [gsb2_t[k_], m82_t[k_]], [nmk2_t[k_]])
            ts(nmk2[k_][:], nmk2[k_][:], -1.0, BIG, ALU.add, ALU.mult, [nmk2_t[k_]], [nmk2_t[k_]])
            kb.op("dve", lambda e: e.memset(nmk2[k_][:, cur:cur + 1], 0.0), (), [nmk2_t[k_]])
            if cur + 1 < 32:
                kb.op("dve", lambda e: e.memset(nmk2[k_][:, cur + 1:32], -BIG), (), [nmk2_t[k_]])

        def gate2(it, k_):
            hh, g = items[it]
            pb = 5 if hh % 2 == 0 else 7
            kb.group("pe", [lambda e: e.transpose(P[pb][0:32, g * 128:(g + 1) * 128], nmk2[k_][:], cst[:, IDN:IDN + 128])],
                     [nmk2_t[k_], cst_t], [P_t[pb]])
            if g == 3:
                cp(nmT[0:32, hh, :], P[pb][0:32, :], [P_t[pb]], [nmT_t[hh]])

        it = 0
        for bi, bf_ in enumerate(blocks):
            mine = []
            while it < len(items) and len(mine) < 2:
                mine.append(it)
                it += 1
            for k_, i_ in enumerate(mine):
                gate1(i_, k_)
            bf_()
            for k_, i_ in enumerate(mine):
                gate2(i_, k_)
        assert it == len(items)
        if STOP < 2.6:
            return
        kb.dma("sp", s_V[tok0:tok0 + TT_, :].rearrange("(g p) d -> p g d", p=128), Vt[:], reads=[Vt_t], writes=[sV_t[t]])

        if STOP < 3:
            return
        gla(t)
        if STOP < 4:
            return
        moba(t)
        if STOP < 5:
            return

        for hh in range(4):
            s_ = hh % 2
            act(sq[s_][:], gqk[:, hh, :], AF.Square, [gqk_t[hh]], [sq_t[s_]])
            mms(P[6][:], [(ones_bf, sq[s_][:])], [sq_t[s_], cbf_t], [P_t[6]])
            act(tmp[s_][:], P[6][:], AF.Ln, [P_t[6], small_t], [tmp_t[s_]], bias=small[:, 0:1], scale=1.0 / 128.0)
            act(tmp[s_][:], tmp[s_][:], AF.Exp, [tmp_t[s_]], [tmp_t[s_]], scale=-0.5)
            stt(tmp[s_][:], gqk[:, hh, :], glan_s[:, 0:1], tmp[s_][:], ALU.mult, ALU.mult, [gqk_t[hh], glan_t, tmp_t[s_]], [tmp_t[s_]])
            tt(oab(hh), tmp[s_][:], gogs[:, hh, :], ALU.mult, [tmp_t[s_], gogs_t[hh]], [oab_t(hh)])
        w_a = wslot()
        wload((wsl[w_a][:, 0:4096], wsl_t[w_a]), ("br", 0), wbr[0], s_br[0])
        w_b = wslot()
        wload((wsl[w_b][:, 0:4096], wsl_t[w_b]), ("br", 1), wbr[1], s_br[1])
        for c in range(8):
            pa, pb = c % 2, 2 + c % 2
            mms(P[pa][:], [(wsl[w_a][:, kc * 1024 + c * 128:kc * 1024 + c * 128 + 128], oab(kc)) for kc in range(4)],
                [wsl_t[w_a]] + [oab_t(k_) for k_ in range(4)], [P_t[pa]])
            mms(P[pb][:], [(wsl[w_b][:, kc * 1024 + c * 128:kc * 1024 + c * 128 + 128], obT[:, kc, :]) for kc in range(4)],
                [wsl_t[w_b]] + obT_t, [P_t[pb]])
            s_ = c % 2
            tt(tmp[s_][:], P[pa][:], sga(c), ALU.mult, [P_t[pa], sga_t(c)], [tmp_t[s_]])
            tt(sg[s_][:], P[pb][:], sgb(c), ALU.mult, [P_t[pb], sgb_t(c)], [sg_t[s_]])
            tt(merged(c), tmp[s_][:], sg[s_][:], ALU.add, [tmp_t[s_], sg_t[s_]], [merged_t(c)], eng="pool")
        for blk in range(2):
            w_ = wslot()
            wload((wsl[w_][:, 0:4096], wsl_t[w_]), ("out", blk), wout[blk], s_out[blk])
            for j in range(4):
                c = blk * 4 + j
                pd = 4 + c % 2
                mms(P[pd][:], [(wsl[w_][:, kc * 512 + j * 128:kc * 512 + j * 128 + 128], merged(kc)) for kc in range(8)],
                    [wsl_t[w_]] + [merged_t(k_) for k_ in range(8)], [P_t[pd]])
                stt(h[:, c, :], P[pd][:], vecs[:, V_HG[1] + c:V_HG[1] + c + 1], h[:, c, :], ALU.mult, ALU.add,
                    [P_t[pd], vecs_t, h_t[c]], [h_t[c]])
                if c >= 1:
                    stat_chunk(c - 1)
        stat_chunk(7)

    def gla(t):
        sc16 = -1.0 / GLA_TAU
        for g in range(4):
            mms(P[4][:, 0:256], [(glrT[:, g * 128:(g + 1) * 128], wlr_s[:])], [glrT_t, wlr_t], [P_t[4]])
            tt(la[:, g, :], P[4][:, 0:256], blrt[:], ALU.add, [P_t[4], blrt_t], [la_t[g]])
            act(la[:, g, :], la[:, g, :], AF.Exp, [la_t[g]], [la_t[g]], scale=-1.0)
            act(la[:, g, :], la[:, g, :], AF.Ln, [la_t[g], small_t], [la_t[g]], bias=small[:, 1:2], scale=1.0)
        for pr in range(2):
            for g in range(4):
                mms(P[5][:, g * 128:(g + 1) * 128], [(la[:, g, pr * 128:(pr + 1) * 128], cst[:, TRI:TRI + 128])],
                    [la_t[g], cst_t], [P_t[5]])
            act(e12[0][:], P[5][:], AF.Exp, [P_t[5]], [e12_t[0]], scale=sc16)
            act(e12[1][:], P[5][:], AF.Exp, [P_t[5]], [e12_t[1]], scale=-sc16)
            stt(qe[:, pr, :], gqk[:, pr, :], 64.0 ** -0.5, e12[0][:], ALU.mult, ALU.mult, [gqk_t[pr], e12_t[0]], [qe_t[pr]])
            tt(ki[:, pr, :], gqk[:, 2 + pr, :], e12[1][:], ALU.mult, [gqk_t[2 + pr], e12_t[1]], [ki_t[pr]])
            for g4 in range(4):
                cp(ebl[:, pr, g4:g4 + 1], e12[0][:, g4 * 128 + 127:g4 * 128 + 128], [e12_t[0]], [ebl_t[pr]])
        for g in range(4):
            mms(P[6][:, 0:256], [(cst[:, TRI:TRI + 128], la[:, g, :])], [la_t[g], cst_t], [P_t[6]])
            mms(P[7][:, 0:256], [(cst[:, LBK:LBK + 128], la[:, g, :])], [la_t[g], cst_t], [P_t[7]])
            cp(bcs[:], P[6][:, 0:256], [P_t[6]], [bcs_t])
            tt(bcs[:], P[7][:, 0:256], bcs[:], ALU.subtract, [P_t[7], bcs_t], [bcs_t])
            act(bcs[:], bcs[:], AF.Exp, [bcs_t], [bcs_t], scale=sc16)
            tt(kd[:, g, :], ktm[:, g, :], bcs[:], ALU.mult, [ktm_t, bcs_t], [kd_t[g]])
        for g in range(4):
            cols = slice(g * 128, (g + 1) * 128)
            for hp in range(2):
                hs = (2 * hp, 2 * hp + 1)
                pr = hp
                R = [(hh % 2) * 64 for hh in hs]
                for s_, hh in enumerate(hs):
                    r0 = R[s_]
                    mms(P[4 + s_][:, 0:128], [(ki[r0:r0 + 64, pr, cols], qe[r0:r0 + 64, pr, cols])], [ki_t[pr], qe_t[pr]], [P_t[4 + s_]])
                for s_, hh in enumerate(hs):
                    mms(P[2 + s_][:, 0:128], [(kd[:, g, pr * 128:(pr + 1) * 128], vtm[:, g, hh * 128:(hh + 1) * 128])],
                        [kd_t[g], vtm_t], [P_t[2 + s_]])
                for s_, hh in enumerate(hs):
                    tt(attn[s_][:], P[4 + s_][:, 0:128], cst[:, M2C:M2C + 128], ALU.mult, [P_t[4 + s_], cst_t], [attn_t[s_]])
                for s_, hh in enumerate(hs):
                    r0 = R[s_]
                    mms(P[6 + s_][:, 0:128], [(vtm[:, g, hh * 128:(hh + 1) * 128], attn[s_][:]),
                                               (Sb[r0:r0 + 64, hh, :], qe[r0:r0 + 64, pr, cols])],
                        [vtm_t, attn_t[s_], Sb_t[hh], qe_t[pr]], [P_t[6 + s_]])
                for s_, hh in enumerate(hs):
                    r0 = R[s_]
                    stt(Sm[r0:r0 + 64, hh, :], Sm[r0:r0 + 64, hh, :], ebl[r0:r0 + 64, pr, g:g + 1], P[2 + s_][r0:r0 + 64, 0:128],
                        ALU.mult, ALU.add, [Sm_t[hh], ebl_t[pr], P_t[2 + s_]], [Sm_t[hh]])
                for s_, hh in enumerate(hs):
                    r0 = R[s_]
                    cp(Sb[r0:r0 + 64, hh, :], Sm[r0:r0 + 64, hh, :], [Sm_t[hh]], [Sb_t[hh]], eng="act_copy")
                for s_, hh in enumerate(hs):
                    cp(gqk[:, hh, cols], P[6 + s_][:, 0:128], [P_t[6 + s_]], [gqk_t[hh]], eng="act_copy")

    def moba(t):
        LAG = 1
        for hp in range(2):
            heads = (2 * hp, 2 * hp + 1)
            kvslot = {}

            def load_kv(kt, hp=hp, kvslot=kvslot):
                if kt >= t or kt in kvslot:
                    return
                s_ = kvc[0] % NKV
                kvc[0] += 1
                kvslot[kt] = s_
                kb.dma("sp", kvK[s_][:], s_K[:, 2 * hp:2 * hp + 2, kt * TT_:(kt + 1) * TT_], reads=[sK_t[kt]], writes=[kvK_t[s_]])
                kb.dma("sp", kvV[s_][:], s_V[kt * TT_:(kt + 1) * TT_, hp * 256:(hp + 1) * 256].rearrange("(g p) d -> p g d", p=128),
                       reads=[sV_t[kt]], writes=[kvV_t[s_]])

            units = [(kt, cc) for kt in range(t + 1) for cc in range(4)]
            n = len(units)
            SB_ = [4, 5, 6, 7]

            def srcs(kt, kvslot=kvslot):
                if kt == t:
                    return (lambda hh, cc: kT[:, hh, cc * 128:(cc + 1) * 128],
                            lambda hh, cc: Vt[:, cc, hh * 128:(hh + 1) * 128], [kT_t, Vt_t])
                s_ = kvslot[kt]
                return (lambda hh, cc, s_=s_: kvK[s_][:, hh % 2, cc * 128:(cc + 1) * 128],
                        lambda hh, cc, s_=s_: kvV[s_][:, cc, (hh % 2) * 128:(hh % 2) * 128 + 128], [kvK_t[s_], kvV_t[s_]])

            def stage1(i):
                kt, cc = units[i]
                if cc == 0:
                    load_kv(kt)
                    load_kv(kt + 1)
                Ksrc, Vsrc, kv_reads = srcs(kt)
                jc = kt * 4 + cc
                blk = jc // 2
                rel = jc - 4 * t
                near = rel >= -1
                banks = [SB_[(2 * i + hi) % 4] for hi in range(2)]
                fns = []
                for hi, hh in enumerate(heads):
                    fns.append(lambda e, hi=hi, hh=hh: e.matmul(P[banks[hi]][:], Ksrc(hh, cc), qT[:, hh, :], start=True, stop=False))
                    fns.append(lambda e, hi=hi, hh=hh: e.matmul(P[banks[hi]][:], esel[:, blk * 128:(blk + 1) * 128], nmT[:, hh, :], start=False, stop=True))
                mm_raw(fns, kv_reads + [qT_t[heads[0]], qT_t[heads[1]], esel_t, nmT_t[heads[0]], nmT_t[heads[1]]],
                       [P_t[banks[0]], P_t[banks[1]]])
                for hi, hh in enumerate(heads):
                    p_ = (2 * i + hi) % 6
                    sp_ = banks[hi]
                    if near:
                        w0 = hh * 1024 + 384 - 128 * rel
                        tt(ssb[hi][:], P[sp_][:], base[:, w0:w0 + 512], ALU.add, [P_t[sp_], base_t], [ssb_t[hi]])
                        act(pt[p_][:], ssb[hi][:], AF.Exp, [ssb_t[hi]], [pt_t[p_]])
                    else:
                        act(pt[p_][:], P[sp_][:], AF.Exp, [P_t[sp_], b31_t], [pt_t[p_]], bias=b31s[:, hh:hh + 1], scale=1.0)

            def stage2(i):
                kt, cc = units[i]
                Ksrc, Vsrc, kv_reads = srcs(kt)
                first = (kt == 0 and cc == 0)
                last = (kt == t and cc == 3)
                ps_ = [(2 * i + hi) % 6 for hi in range(2)]
                fns = []
                for hi, hh in enumerate(heads):
                    fns.append(lambda e, hi=hi, hh=hh: e.matmul(P[hi][:], Vsrc(hh, cc), pt[ps_[hi]][:], start=first, stop=last))
                    fns.append(lambda e, hi=hi: e.matmul(P[2 + hi][:], ones_bf, pt[ps_[hi]][:], start=first, stop=last))
                mm_raw(fns, kv_reads + [pt_t[ps_[0]], pt_t[ps_[1]], cbf_t], [P_t[0], P_t[1], P_t[2], P_t[3]])

            for i in range(n + LAG):
                if i < n:
                    stage1(i)
                if i >= LAG:
                    stage2(i - LAG)
            for hi, hh in enumerate(heads):
                kb.op("dve", lambda e, hi=hi: e.reciprocal(out=den[:], in_=P[2 + hi][:]), [P_t[2 + hi]], [den_t])
                tt(obT[:, hh, :], P[hi][:], den[:], ALU.mult, [P_t[hi], den_t], [obT_t[hh]])

    kvc = [0]

    for t in range(NT):
        cur_t[0] = t
        tok0 = t * TT_
        h, h_t = hb[t % 2], hb_t[t % 2]
        if STOP >= 1:
            ffn(0, 0)
        if t + 1 < NT:
            load_x(t + 1)
        if STOP >= 2:
            mixer(t)
        if STOP >= 6:
            ffn(2, 1, pre=True)
        stats_finish()
        for c in range(8):
            stt(h[:, c, :], h[:, c, :], gn[:, 24 + c:24 + c + 1], rstd[:], ALU.mult, ALU.mult, [h_t[c], gn_t, rstd_t], [h_t[c]])
            kb.dma("sp", outT[:, c, tok0:tok0 + TT_], h[:, c, :], reads=[h_t[c]], is_output=True)

    kb.finish()
    print("sbuf bytes remaining/partition:", nc.sbuf_bytes_remaining, " instr counts:", kb.cnt)
    kb.close()
    es.close()
    return nc


def _rel_bucket_np(dist):
    n = np.maximum(dist, 0)
    nf = np.maximum(n, 1).astype(np.float32)
    large = 16 + (np.log(nf / np.float32(16)) / np.float32(math.log(128 / 16)) * np.float32(16)).astype(np.int32)
    large = np.minimum(large, 31)
    return np.where(n < 16, n, large)


def _host_inputs(inp, b, NT):
    S = NT * TT_
    f = np.float32
    x = np.asarray(inp["x"], f)[b, :S]
    m = {}
    m["xT"] = np.ascontiguousarray(x.T.reshape(8, 128, S).transpose(1, 0, 2))
    m["cT"] = np.ascontiguousarray(np.asarray(inp["c"], f)[b].reshape(8, 128).T)
    wa = np.asarray(inp["w_ada"], f)[0]
    m["w_ada"] = np.ascontiguousarray(wa.reshape(8, 128, 36, 256).transpose(2, 1, 0, 3).reshape(36, 128, 2048))
    m["b_ada"] = np.ascontiguousarray(np.asarray(inp["b_ada"], f)[0].reshape(72, 128).T)
    gs = [np.asarray(inp[k], f).reshape(-1) for k in ("norm_ff1", "norm_mix", "norm_ff2", "norm_final")]
    m["gains"] = np.ascontiguousarray(np.concatenate([g.reshape(8, 128).T for g in gs], axis=1))
    ffw = ((inp["w_ff1_gate"], inp["w_ff1_up"], inp["w_ff1_down"]), (inp["w_ff2_gate"], inp["w_ff2_up"], inp["w_ff2_down"]))
    for i in range(2):
        wg = np.asarray(ffw[i][0], f)[0]
        wu = np.asarray(ffw[i][1], f)[0]
        wd = np.asarray(ffw[i][2], f)[0]
        g4 = wg.reshape(8, 128, 11, 256).transpose(2, 1, 0, 3)
        u4 = wu.reshape(8, 128, 11, 256).transpose(2, 1, 0, 3)
        m["wgu%d" % i] = np.ascontiguousarray(np.stack([g4, u4], axis=2).reshape(11, 128, 4096))
        m["wdn%d" % i] = np.ascontiguousarray(wd.reshape(NFC, 128, 8, 128).transpose(2, 1, 0, 3).reshape(8, 128, NFC * 128))
    wi = np.asarray(inp["w_in"], f)[0]

    def cols(a0, n):
        return wi[:, a0:a0 + n]

    def blk512(wc):
        return wc.reshape(8, 128, 512).transpose(1, 0, 2).reshape(128, 4096)

    fm = [cols(OFF["mq"], 512), cols(OFF["mk"], 512), np.concatenate([cols(OFF["gq"], 256), cols(OFF["gk"], 256)], axis=1),
          cols(OFF["gog"], 512), cols(OFF["ga"], 512), cols(OFF["ga"] + 512, 512), cols(OFF["gb"], 512), cols(OFF["gb"] + 512, 512)]
    m["win_fm"] = np.ascontiguousarray(np.stack([blk512(w_) for w_ in fm]))
    gk_pad = np.concatenate([cols(OFF["gk"], 256), np.zeros((1024, 256), f)], axis=1)
    m["win_tm"] = np.ascontiguousarray(np.stack([blk512(cols(OFF["mv"], 512)), blk512(cols(OFF["gv"], 512)), blk512(gk_pad)]))
    m["win_lr"] = np.ascontiguousarray(cols(OFF["glr"], 16).reshape(8, 128, 16).transpose(1, 0, 2).reshape(128, 128))
    m["wlr"] = np.ascontiguousarray(np.asarray(inp["w_gla_lr"], f)[0])
    blr = np.asarray(inp["b_gla_lr"], f)[0]
    m["blr_fm"] = np.ascontiguousarray(blr.reshape(2, 128).T)
    m["blr_tm"] = np.ascontiguousarray(np.broadcast_to(blr[None, :], (128, 256)))
    m["glan"] = np.ascontiguousarray(np.asarray(inp["gla_norm"], f)[0].reshape(128, 1))
    brg = np.asarray(inp["w_br_gla"], f)[0]
    brm = np.asarray(inp["w_br_moba"], f)[0]
    m["wbr"] = np.ascontiguousarray(np.stack([w_.reshape(4, 128, 1024).transpose(1, 0, 2).reshape(128, 4096) for w_ in (brg, brm)]))
    wo = np.asarray(inp["w_out"], f)[0]
    m["wout"] = np.ascontiguousarray(wo.reshape(8, 128, 2, 512).transpose(2, 1, 0, 3).reshape(2, 128, 4096))
    rb = np.asarray(inp["rel_bias"], f)
    kk = np.arange(128)[:, None]
    jj = np.arange(1024)[None, :]
    dist = jj - 384 - kk
    bidx = _rel_bucket_np(dist)
    basev = rb[:, bidx]
    basev = np.where(dist[None] >= 0, basev, f(-BIG))
    m["relbase"] = np.ascontiguousarray(basev.transpose(1, 0, 2).reshape(128, 4096).astype(f))
    m["b31"] = np.ascontiguousarray(np.broadcast_to(rb[:, 31][None, :], (128, 4)).astype(f))
    ii = np.arange(128)
    ident = np.eye(128, dtype=f)
    tri = (ii[:, None] <= ii[None, :]).astype(f)
    lblk = np.ones((128, 128), f)
    m["consts"] = np.ascontiguousarray(np.concatenate([ident, tri, lblk, tri, np.ones((128, 128), f)], axis=1))
    es_ = np.zeros((128, 32, 128), f)
    for n in range(32):
        es_[n, n, :] = 1.0
    m["esel"] = np.ascontiguousarray(es_.reshape(128, 32 * 128))
    return m


_PROG_CACHE = {}


def run(inputs, NT=NT_FULL, n_cores=N_CORES, spread=False):
    if NT not in _PROG_CACHE:
        _PROG_CACHE[NT] = build_program(NT)
    nc = _PROG_CACHE[NT]
    real = [_host_inputs(inputs, b, NT) for b in range(n_cores)]
    if spread:
        zero = {k: np.zeros_like(v) for k, v in real[0].items()}
        in_maps = []
        for b in range(n_cores):
            in_maps.append(real[b])
            in_maps.append(zero)
        slots = [2 * b for b in range(n_cores)]
    else:
        in_maps = real
        slots = list(range(n_cores))
    res = run_bass_kernel_spmd(nc, in_maps, core_ids=list(range(len(in_maps))))
    S = NT * TT_
    outs = []
    for b in range(n_cores):
        oT = np.asarray(res.results[slots[b]]["outT"])
        outs.append(oT.transpose(2, 1, 0).reshape(S, D))
    return np.stack(outs).astype(np.float32)


def kernel(**inputs):
    return run(inputs, spread=True)
```
